# Optimizing a Trainium2 kernel written in Bass

```python
import math
import jax, jax.numpy as jnp
from jax import lax
import numpy as np

D_MODEL = 1024
BATCH = 8
SEQ = 4096
DEPTH = 2

GRID_W = 64
CTX_LEN = 256
N_EVEN = (DEPTH + 1) // 2
N_ODD = DEPTH // 2
LAST_EVEN = (DEPTH - 1) - (DEPTH - 1) % 2

EPS = 1e-6
N_MOD = 9
D_FF = 2816

A_HEADS = D_MODEL // 256
A_HEAD_DIM = 64
A_V_DIM = 2 * A_HEAD_DIM
A_WIDTH = A_HEADS * A_V_DIM
Q_BLOCK = 128
ROPE_BASE = 10000.0
ROPE_FREQS = A_HEAD_DIM // 4

B_GROUPS = D_MODEL // 256
B_GROUP_CH = 128
B_WIDTH = B_GROUPS * B_GROUP_CH
B_CHUNK = 128

Q_COLS = A_HEADS * 2 * A_HEAD_DIM
K_COLS = A_HEADS * 2 * A_HEAD_DIM
V_COLS = A_HEADS * A_V_DIM
Z_COLS = 2 * B_WIDTH
IN_COLS = Q_COLS + K_COLS + V_COLS + Z_COLS
MIX_WIDTH = A_WIDTH + B_WIDTH

C_GROUPS = 4
C_GROUP_CH = D_MODEL // C_GROUPS

kernel_name = "hybrid_diffattn_gmlp_fnet_macaron_dit"


def rmsnorm(x, g=None):
    xf = x.astype(jnp.float32)
    y = xf * lax.rsqrt(jnp.mean(xf * xf, axis=-1, keepdims=True) + EPS)
    if g is not None:
        y = y * g.astype(jnp.float32)
    return y.astype(x.dtype)


def layernorm(x, g, b):
    xf = x.astype(jnp.float32)
    mu = jnp.mean(xf, axis=-1, keepdims=True)
    var = jnp.mean(jnp.square(xf - mu), axis=-1, keepdims=True)
    y = (xf - mu) * lax.rsqrt(var + EPS) * g.astype(jnp.float32) + b.astype(jnp.float32)
    return y.astype(x.dtype)


def modulate(x, shift, scale):
    return rmsnorm(x) * (1 + scale) + shift


def swiglu(h, w_gu, w_down):
    g, u = jnp.split(h @ w_gu, 2, axis=-1)
    return (jax.nn.silu(g) * u) @ w_down


def axial_angles(L):
    rows = L // GRID_W
    row = jnp.repeat(jnp.arange(rows), GRID_W)
    col = jnp.tile(jnp.arange(GRID_W), rows)
    inv = ROPE_BASE ** (-jnp.arange(ROPE_FREQS, dtype=jnp.float32) / ROPE_FREQS)
    ang = jnp.stack([row[:, None] * inv, col[:, None] * inv], axis=1)
    return jnp.cos(ang), jnp.sin(ang)


def apply_axial_rope(x, cos, sin):
    shp = x.shape
    xr = x.reshape(shp[:-1] + (2, 2, ROPE_FREQS))
    xa, xb = xr[..., 0, :], xr[..., 1, :]
    c = cos[None, :, None, None].astype(x.dtype)
    s = sin[None, :, None, None].astype(x.dtype)
    out = jnp.stack([xa * c - xb * s, xb * c + xa * s], axis=-2)
    return out.reshape(shp)


def diff_attend(q, k, v, lam):
    s = jnp.einsum('bqhmd,bkhmd->bhmqk', q, k, preferred_element_type=jnp.float32) * (A_HEAD_DIM ** -0.5)
    p = jax.nn.softmax(s, axis=-1)
    a = p[:, :, 0] - lam * p[:, :, 1]
    return jnp.einsum('bhqk,bkhe->bqhe', a.astype(v.dtype), v)


def split_qkvz(proj):
    lead = proj.shape[:-1]
    q = proj[..., :Q_COLS].reshape(lead + (A_HEADS, 2, A_HEAD_DIM))
    k = proj[..., Q_COLS:Q_COLS + K_COLS].reshape(lead + (A_HEADS, 2, A_HEAD_DIM))
    v = proj[..., Q_COLS + K_COLS:Q_COLS + K_COLS + V_COLS].reshape(lead + (A_HEADS, A_V_DIM))
    z = proj[..., Q_COLS + K_COLS + V_COLS:]
    return q, k, v, z


def diff_head_out(o, subln_g, lam_init):
    y = rmsnorm(o, subln_g) * (1 - lam_init)
    return y.reshape(o.shape[:2] + (A_WIDTH,))


def spatial_gating(z, norm_g, norm_b, w_s, b_s):
    Bn, L, _ = z.shape
    u, vg = jnp.split(jax.nn.gelu(z), 2, axis=-1)
    vg = layernorm(vg.reshape(Bn, L, B_GROUPS, B_GROUP_CH), norm_g, norm_b)
    vg = vg.reshape(Bn, L // B_CHUNK, B_CHUNK, B_GROUPS, B_GROUP_CH)
    mixed = jnp.einsum('gpq,bnqgc->bnpgc', w_s, vg) + b_s.T[:, :, None]
    return u * mixed.reshape(Bn, L, B_WIDTH)


def even_mixer(h, hc, w_in, w_out, lam_vec, subln_g, sgu_g, sgu_bn, sgu_w, sgu_b, lam_init, cos, sin, ctx_full):
    Bn, L, _ = h.shape
    lv = lam_vec.astype(jnp.float32)
    lam = jnp.exp(jnp.sum(lv[0] * lv[1])) - jnp.exp(jnp.sum(lv[2] * lv[3])) + lam_init
    q, k, v, z = split_qkvz(h @ w_in)
    q = apply_axial_rope(q, cos, sin)
    k = apply_axial_rope(k, cos, sin)
    if ctx_full:
        qc, kc, vc, zc = split_qkvz(hc @ w_in)
    else:
        kv = hc @ w_in[:, Q_COLS:Q_COLS + K_COLS + V_COLS]
        kc = kv[..., :K_COLS].reshape(kv.shape[:2] + (A_HEADS, 2, A_HEAD_DIM))
        vc = kv[..., K_COLS:].reshape(kv.shape[:2] + (A_HEADS, A_V_DIM))
    k_all = jnp.concatenate([k, kc], axis=1)
    v_all = jnp.concatenate([v, vc], axis=1)
    qb = q.reshape(Bn, L // Q_BLOCK, Q_BLOCK, A_HEADS, 2, A_HEAD_DIM).swapaxes(0, 1)
    o = lax.map(lambda qi: diff_attend(qi, k_all, v_all, lam), qb)
    o = o.swapaxes(0, 1).reshape(Bn, L, A_HEADS, A_V_DIM)
    y = jnp.concatenate([diff_head_out(o, subln_g, lam_init),
                         spatial_gating(z, sgu_g, sgu_bn, sgu_w, sgu_b)], axis=-1) @ w_out
    yc = None
    if ctx_full:
        oc = diff_attend(qc, kc, vc, lam)
        yc = jnp.concatenate([diff_head_out(oc, subln_g, lam_init),
                              spatial_gating(zc, sgu_g, sgu_bn, sgu_w, sgu_b)], axis=-1) @ w_out
    return y, yc


def fourier_mixer(h, w_out, b_out):
    Bn, L, _ = h.shape
    hg = h.astype(jnp.float32).reshape(Bn, L, C_GROUPS, C_GROUP_CH)
    f = jnp.fft.fftn(hg, axes=(1, 3), norm='ortho').real
    return f.reshape(Bn, L, D_MODEL).astype(h.dtype) @ w_out + b_out


def setup_inputs(seed: int = 0) -> dict:
    key = jax.random.key(seed)
    ks = jax.random.split(key, 20)
    nrm = lambda k, shp, s: jax.random.normal(k, shp, jnp.float32) * s
    return {
        "x": nrm(ks[0], (BATCH, SEQ, D_MODEL), 1.0),
        "c": nrm(ks[1], (BATCH, D_MODEL), 1.0),
        "ctx": nrm(ks[2], (BATCH, CTX_LEN, D_MODEL), 1.0),
        "c_ctx": nrm(ks[3], (D_MODEL,), 1.0),
        "ada_w": nrm(ks[4], (DEPTH, D_MODEL, N_MOD * D_MODEL), D_MODEL ** -0.5),
        "ada_b": nrm(ks[5], (DEPTH, N_MOD * D_MODEL), 0.02),
        "ffn_w_gu": nrm(ks[6], (DEPTH, 2, D_MODEL, 2 * D_FF), D_MODEL ** -0.5),
        "ffn_w_down": nrm(ks[7], (DEPTH, 2, D_FF, D_MODEL), D_FF ** -0.5),
        "mix_w_in": nrm(ks[8], (N_EVEN, D_MODEL, IN_COLS), D_MODEL ** -0.5),
        "mix_w_out": nrm(ks[9], (N_EVEN, MIX_WIDTH, D_MODEL), MIX_WIDTH ** -0.5),
        "diff_lambda": nrm(ks[10], (N_EVEN, 4, A_HEAD_DIM), 0.1),
        "diff_subln_g": 1.0 + nrm(ks[11], (N_EVEN, A_V_DIM), 0.02),
        "sgu_norm_g": 1.0 + nrm(ks[12], (N_EVEN, B_GROUPS, B_GROUP_CH), 0.02),
        "sgu_norm_b": nrm(ks[13], (N_EVEN, B_GROUPS, B_GROUP_CH), 0.02),
        "sgu_w": nrm(ks[14], (N_EVEN, B_GROUPS, B_CHUNK, B_CHUNK), B_CHUNK ** -0.5),
        "sgu_b": 1.0 + nrm(ks[15], (N_EVEN, B_GROUPS, B_CHUNK), 0.02),
        "fourier_w_out": nrm(ks[16], (N_ODD, D_MODEL, D_MODEL), D_MODEL ** -0.5),
        "fourier_b_out": nrm(ks[17], (N_ODD, D_MODEL), 0.02),
        "final_norm_g": 1.0 + nrm(ks[18], (D_MODEL,), 0.02),
    }


def reference(x, c, ctx, c_ctx, ada_w, ada_b, ffn_w_gu, ffn_w_down, mix_w_in, mix_w_out, diff_lambda,
              diff_subln_g, sgu_norm_g, sgu_norm_b, sgu_w, sgu_b, fourier_w_out, fourier_b_out, final_norm_g):
    cos, sin = axial_angles(x.shape[1])
    cx = ctx
    for l in range(DEPTH):
        is_even = l % 2 == 0
        ctx_reads = l <= LAST_EVEN
        ctx_full = l < LAST_EVEN
        m = jnp.split((jax.nn.silu(c) @ ada_w[l] + ada_b[l])[:, None, :], N_MOD, axis=-1)
        x = x + 0.5 * m[2] * swiglu(modulate(x, m[0], m[1]), ffn_w_gu[l, 0], ffn_w_down[l, 0])
        if ctx_reads:
            mc = jnp.split(jax.nn.silu(c_ctx) @ ada_w[l] + ada_b[l], N_MOD, axis=-1)
            cx = cx + 0.5 * mc[2] * swiglu(modulate(cx, mc[0], mc[1]), ffn_w_gu[l, 0], ffn_w_down[l, 0])
        h = modulate(x, m[3], m[4])
        if is_even:
            e = l // 2
            lam_init = 0.8 - 0.6 * math.exp(-0.3 * l)
            hc = modulate(cx, mc[3], mc[4])
            y, yc = even_mixer(h, hc, mix_w_in[e], mix_w_out[e], diff_lambda[e], diff_subln_g[e],
                               sgu_norm_g[e], sgu_norm_b[e], sgu_w[e], sgu_b[e], lam_init, cos, sin, ctx_full)
            if ctx_full:
                cx = cx + mc[5] * yc
        else:
            o = l // 2
            y = fourier_mixer(h, fourier_w_out[o], fourier_b_out[o])
            if ctx_full:
                cx = cx + mc[5] * fourier_mixer(modulate(cx, mc[3], mc[4]), fourier_w_out[o], fourier_b_out[o])
        x = x + m[5] * y
        x = x + 0.5 * m[8] * swiglu(modulate(x, m[6], m[7]), ffn_w_gu[l, 1], ffn_w_down[l, 1])
        if ctx_full:
            cx = cx + 0.5 * mc[8] * swiglu(modulate(cx, mc[6], mc[7]), ffn_w_gu[l, 1], ffn_w_down[l, 1])
    return rmsnorm(x, final_norm_g)
```

```python
import math
import numpy as np
import ml_dtypes
from contextlib import ExitStack
import concourse.bass as bass
import concourse.mybir as mybir
from concourse.bass_utils import run_bass_kernel_spmd

F32 = mybir.dt.float32
BF16 = mybir.dt.bfloat16
AF = mybir.ActivationFunctionType
ALU = mybir.AluOpType

N, D, CTX, FF = 4096, 1024, 256, 2816
TT = 256
NT = N // TT
EPS = 1e-6
NKC = 34


class Reg:
    __slots__ = ("name", "ws", "rs")

    def __init__(self, name=""):
        self.name = name
        self.ws = []
        self.rs = []


class Op:
    __slots__ = ("eng", "fn", "deps", "sig", "ms", "dma", "dsem", "dval", "phase")


class Prog:
    ENGS = ("pe", "act", "dve", "pool", "sp")
    NDSEM = 40
    NDSEM_SW = 12

    def __init__(self, nc, es, same_engine_sync=True):
        self.nc = nc
        self.same = same_engine_sync
        self.esem = {e: es.enter_context(nc.semaphore("s_" + e)) for e in self.ENGS}
        self.dsems = [es.enter_context(nc.semaphore("d%d" % i)) for i in range(self.NDSEM + self.NDSEM_SW)]
        self.ms_count = {e: 0 for e in self.ENGS}
        self.seen = {e: {f: 0 for f in self.ENGS} for e in self.ENGS}
        self.seen_d = {e: [0] * (self.NDSEM + self.NDSEM_SW) for e in self.ENGS}
        self.ndma = 0
        self.dma_hist = []
        self.ndma_sw = 0
        self.dma_hist_sw = []
        self.ops = []
        self.n_inst = 0
        self.n_wait = 0
        self.phase = 0

    def op(self, eng, fn, reads=(), writes=(), dma=False, relax=False):
        o = Op()
        o.eng = eng
        o.fn = fn
        o.sig = False
        o.ms = None
        o.dma = dma
        o.phase = self.phase
        deps = []
        raw_ids = set()
        for r in reads:
            for wr in r.ws:
                deps.append(wr)
                raw_ids.add(id(wr))
        for w in writes:
            deps.extend(w.ws)
            deps.extend(w.rs)
        if dma and eng == "pool":
            j = self.ndma_sw
            self.ndma_sw += 1
            o.dsem = self.NDSEM + j % self.NDSEM_SW
            o.dval = 16 * (j // self.NDSEM_SW + 1)
            if j >= self.NDSEM_SW:
                deps.append(self.dma_hist_sw[j - self.NDSEM_SW])
            self.dma_hist_sw.append(o)
        elif dma:
            j = self.ndma
            self.ndma += 1
            o.dsem = j % self.NDSEM
            o.dval = 16 * (j // self.NDSEM + 1)
            if j >= self.NDSEM:
                deps.append(self.dma_hist[j - self.NDSEM])
            self.dma_hist.append(o)
        dd = []
        ids = set()
        for d in deps:
            if id(d) in ids or d is o or d.phase < self.phase:
                continue
            ids.add(id(d))
            if (not d.dma) and d.eng == eng and (eng == "pe" or not self.same or (relax and id(d) not in raw_ids)):
                continue
            dd.append(d)
            if not d.dma:
                d.sig = True
        o.deps = dd
        for r in reads:
            r.rs.append(o)
        for w in writes:
            if w.rs:
                w.ws = [o]
                w.rs = []
            else:
                if not dma:
                    w.ws = [x for x in w.ws if x.dma or x.eng != eng]
                w.ws.append(o)
        self.ops.append(o)
        return o

    def barrier(self):
        lasts = []
        for e in self.ENGS:
            for o in reversed(self.ops):
                if o.eng == e and not o.dma and o.fn is not None:
                    lasts.append(o)
                    break
        latest = {}
        for o in self.ops:
            if o.dma:
                latest[o.dsem] = o
        for e in self.ENGS:
            b = Op()
            b.eng = e
            b.fn = None
            b.sig = False
            b.ms = None
            b.dma = False
            b.phase = self.phase
            b.deps = [l for l in lasts if (l.eng != e or (self.same and e != "pe"))] + list(latest.values())
            for d in b.deps:
                if not d.dma:
                    d.sig = True
            self.ops.append(b)

    def emit(self):
        nc = self.nc
        ops = self.ops
        self.ops = []
        self.phase += 1
        for o in ops:
            if o.sig and not o.dma:
                self.ms_count[o.eng] += 1
                o.ms = self.ms_count[o.eng]
        by_eng = {e: [o for o in ops if o.eng == e] for e in self.ENGS}
        prog = self

        def run(eng_name, engine):
            seen = prog.seen[eng_name]
            seen_d = prog.seen_d[eng_name]
            for o in by_eng[eng_name]:
                for d in o.deps:
                    if d.dma:
                        if seen_d[d.dsem] < d.dval:
                            engine.wait_ge(prog.dsems[d.dsem], d.dval)
                            seen_d[d.dsem] = d.dval
                            prog.n_wait += 1
                    else:
                        if seen[d.eng] < d.ms:
                            engine.wait_ge(prog.esem[d.eng], d.ms)
                            seen[d.eng] = d.ms
                            prog.n_wait += 1
                if o.fn is None:
                    continue
                ins = o.fn(engine)
                prog.n_inst += 1
                if o.dma:
                    ins.then_inc(prog.dsems[o.dsem], 16)
                elif o.sig:
                    ins.then_inc(prog.esem[eng_name], 1)

        with nc.Block() as block:
            @block.tensor
            def _(e):
                run("pe", e)

            @block.scalar
            def _(e):
                run("act", e)

            @block.vector
            def _(e):
                run("dve", e)

            @block.gpsimd
            def _(e):
                run("pool", e)

            @block.sync
            def _(e):
                run("sp", e)


class Ctx:
    pass


def build(stop_after=99, same_sync=True):
    nc = bass.Bass("TRN2", target_bir_lowering=False)
    C = Ctx()
    C.nc = nc

    def din(name, shape, dt=F32):
        return nc.dram_tensor(name, shape, dt, kind="ExternalInput").ap()

    C.x_in = din("x", [N, D])
    C.ctx_in = din("ctx", [CTX, D])
    C.cc_in = din("cc", [128, 8, 2])
    C.ada_w = din("ada_w", [2, 1024, 9216])
    C.ada_bc = din("ada_bc", [2, 128, 72])
    C.w_gu = din("w_gu", [2, 2, 1024, 2 * FF])
    C.w_dn = din("w_dn", [2, 2, FF, 1024])
    C.w_in = din("w_in", [1024, 2560])
    C.w_rot = din("w_rot", [1024, 1024])
    C.w_out = din("w_out", [1024, 1024])
    C.lam_in = din("lam", [1, 256])
    C.subg = din("subg", [1, 128])
    C.sgu_g = din("sgu_g", [1, 512])
    C.sgu_bn = din("sgu_bn", [1, 512])
    C.sgu_wT = din("sgu_wT", [4, 128, 128])
    C.sgu_b = din("sgu_b", [1, 512])
    C.w_fo = din("w_fo", [1024, 1024])
    C.b_fo = din("b_fo", [1, 1024])
    C.fng = din("fng", [1, 1024])
    C.ident_in = din("ident", [128, 128])
    C.rope_c = din("rope_c", [128, N])
    C.rope_s = din("rope_s", [128, N])
    C.dftC = din("dftC", [NT // 2, 128, 32, TT], BF16)
    C.dftS = din("dftS", [NT // 2, 128, 32, TT], BF16)
    C.dft0 = din("dft0", [128, 32, 2], BF16)
    C.ccSn = din("ccSn", [128, 2, 256], BF16)
    C.ccC = din("ccC", [128, 2, 256], BF16)
    C.ccS = din("ccS", [128, 2, 256], BF16)
    C.out = nc.dram_tensor("out", [N, D], F32, kind="ExternalOutput").ap()
    C.cx = nc.dram_tensor("cx_s", [CTX, D], F32).ap()
    C.QTs = nc.dram_tensor("qt_s", [128, 4, N], BF16).ap()
    C.SGs = nc.dram_tensor("sg_s", [128, 4, N], BF16).ap()

    with ExitStack() as es:
        P = Prog(nc, es, same_engine_sync=same_sync)
        C.P = P
        C.pb = [es.enter_context(nc.psum_tensor("pb%d" % i, [128, 512], F32)) for i in range(8)]
        C.pbR = [Reg("pb%d" % i) for i in range(8)]

        cnt = [0]

        def T(name, shape, dt, stack=es):
            cnt[0] += 1
            return stack.enter_context(nc.sbuf_tensor("%s_t%d" % (name, cnt[0]), shape, dt))

        C.T = T
        C.ident = T("ident", [128, 128], F32)
        C.ones = T("ones", [128, 128], F32)
        C.mhalf = T("mhalf", [128, 8], F32)
        C.modcol = [T("modcol%d" % l, [128, 72, 2], F32) for l in range(2)]
        C.sc1p = [T("sc1p%d" % l, [128, 72, 2], F32) for l in range(2)]
        C.R_const = Reg("const")
        C.R_mod = Reg("mod")
        C.outR = [Reg("out%d" % t) for t in range(NT)]

        P.op("sp", lambda e: e.dma_start(out=C.ident[:], in_=C.ident_in[:, :]), writes=[C.R_const], dma=True)
        P.op("pool", lambda e: e.memset(C.ones[:], 1.0), writes=[C.R_const])
        P.op("pool", lambda e: e.memset(C.mhalf[:], -0.5), writes=[C.R_const])

        phase_ada(C)
        plan = [
            lambda: phase_ffn(C, 0, 0, C.x_in, with_ctx=True),
            lambda: phase_mix0(C),
            lambda: phase_ffn(C, 0, 1, C.out),
            lambda: phase_ffn(C, 1, 0, C.out),
            lambda: phase_fourier(C),
            lambda: phase_ffn(C, 1, 1, C.out, final=True),
        ]
        for i, ph in enumerate(plan):
            if i >= stop_after:
                break
            ph()
        print("bass build: inst=%d waits=%d dmas=%d" % (P.n_inst, P.n_wait, P.ndma))
    return nc


def phase_ada(C):
    P, nc = C.P, C.nc
    with ExitStack() as es:
        T = lambda n, s, d: C.T(n, s, d, es)
        cc = T("cc", [128, 8, 2], F32)
        sc = T("sc", [128, 8, 2], BF16)
        abc = [T("abc%d" % l, [128, 72], F32) for l in range(2)]
        NB = 4
        wt = [T("adaw%d" % i, [128, 8, 1024], F32) for i in range(NB)]
        wtR = [Reg("adaw%d" % i) for i in range(NB)]
        wb = [T("adab%d" % i, [128, 8, 1024], BF16) for i in range(2)]
        wbR = [Reg("adab%d" % i) for i in range(2)]
        R_cc, R_sc, R_abc = Reg(), Reg(), Reg()
        P.op("sp", lambda e: e.dma_start(out=cc[:], in_=C.cc_in[:, :, :]), writes=[R_cc], dma=True)
        for l in range(2):
            P.op("sp", lambda e, l=l: e.dma_start(out=abc[l][:], in_=C.ada_bc[l, :, :]), writes=[R_abc], dma=True)
        P.op("act", lambda e: e.activation(out=sc[:], in_=cc[:], func=AF.Silu), reads=[R_cc], writes=[R_sc])
        blk = 0
        for l in range(2):
            wv = C.ada_w[l].rearrange("(kc p) n -> p kc n", p=128)
            for j in range(9):
                b = blk % NB
                b2 = blk % 2
                blk += 1
                for hh in range(4):
                    P.op("sp" if hh % 2 == 0 else "act",
                         lambda e, b=b, j=j, wv=wv, hh=hh: e.dma_start(out=wt[b][:, hh * 2:(hh + 1) * 2, :],
                                                                      in_=wv[:, hh * 2:(hh + 1) * 2, j * 1024:(j + 1) * 1024]),
                         writes=[wtR[b]], dma=True)
                P.op("dve", lambda e, b=b, b2=b2: e.tensor_copy(out=wb[b2][:, 0:3, :], in_=wt[b][:, 0:3, :]), reads=[wtR[b]], writes=[wbR[b2]])
                P.op("act", lambda e, b=b, b2=b2: e.activation(out=wb[b2][:, 3:6, :], in_=wt[b][:, 3:6, :], func=AF.Identity), reads=[wtR[b]], writes=[wbR[b2]])
                P.op("pool", lambda e, b=b, b2=b2: e.tensor_copy(out=wb[b2][:, 6:8, :], in_=wt[b][:, 6:8, :]), reads=[wtR[b]], writes=[wbR[b2]])
                for cchunk in range(8):
                    ci = j * 8 + cchunk
                    pbi = ci % 4
                    for kc in range(8):
                        P.op("pe", lambda e, b2=b2, kc=kc, cchunk=cchunk, pbi=pbi: e.matmul(
                            C.pb[pbi][:, 0:2], lhsT=wb[b2][:, kc, cchunk * 128:(cchunk + 1) * 128], rhs=sc[:, kc, :],
                            start=(kc == 0), stop=(kc == 7)),
                            reads=[wbR[b2], R_sc], writes=[C.pbR[pbi]])
                    P.op("dve", lambda e, l=l, ci=ci, pbi=pbi: e.tensor_scalar(
                        out=C.modcol[l][:, ci, :], in0=C.pb[pbi][:, 0:2], scalar1=abc[l][:, ci:ci + 1], scalar2=None,
                        op0=ALU.add), reads=[C.pbR[pbi], R_abc], writes=[C.R_mod])
        for l in range(2):
            P.op("dve", lambda e, l=l: e.tensor_scalar(out=C.sc1p[l][:], in0=C.modcol[l][:], scalar1=1.0, scalar2=None,
                                                       op0=ALU.add), reads=[C.R_mod], writes=[C.R_mod])
        P.barrier()
        P.emit()


def build_bc(C, dst, dstR, col_ap_fn, mult, pbis=(0, 1), tmp=None, tmpR=None):
    P = C.P
    for half in range(2):
        pbi = pbis[half]
        for q in range(4):
            kc = half * 4 + q
            t = tmp[kc % 2]
            P.op("dve", lambda e, t=t, kc=kc: e.tensor_scalar(out=t[:], in0=C.ones[:], scalar1=col_ap_fn(kc), scalar2=float(mult),
                                                             op0=ALU.mult, op1=ALU.mult),
                 reads=[C.R_mod, C.R_const], writes=[tmpR[kc % 2]])
            P.op("pe", lambda e, t=t, q=q, pbi=pbi: e.matmul(C.pb[pbi][:, q * 128:(q + 1) * 128], lhsT=t[:], rhs=C.ident[:],
                                                             start=True, stop=True),
                 reads=[tmpR[kc % 2], C.R_const], writes=[C.pbR[pbi]])
        P.op("act", lambda e, half=half, pbi=pbi: e.activation(out=dst[:, half * 512:(half + 1) * 512], in_=C.pb[pbi][:, :], func=AF.Identity),
             reads=[C.pbR[pbi]], writes=[dstR])


class Front:
    def __init__(self, C, es, nbuf=3):
        T = lambda n, s, d: C.T(n, s, d, es)
        self.C = C
        self.nbuf = nbuf
        self.xin = [T("xin%d" % i, [128, 2, 1024], F32) for i in range(nbuf)]
        self.xinR = [Reg("xin%d" % i) for i in range(nbuf)]
        self.xn = T("xn", [128, 2, 1024], F32)
        self.xnR = Reg("xn")
        self.hT = [T("hT%d" % i, [128, 8, TT], BF16) for i in range(2)]
        self.hTR = [Reg("hT%d" % i) for i in range(2)]
        self.ss = T("ss", [128, 2], F32)
        self.rstd = T("rstd", [128, 2], F32)
        self.ssR = Reg("ss")
        self.rstdR = Reg("rstd")
        self.cnt = 0

    def load(self, src_rows, b, srcR=None):
        C, P = self.C, self.C.P
        P.op("sp", lambda e: e.dma_start(out=self.xin[b][:], in_=src_rows.rearrange("(s p) d -> p s d", p=128)),
             reads=[srcR] if srcR is not None else [], writes=[self.xinR[b]], dma=True)

    def norm(self, b):
        C, P = self.C, self.C.P
        xin, xn = self.xin[b], self.xn
        for s in range(2):
            P.op("act", lambda e, s=s: e.activation(out=xn[:, s, :], in_=xin[:, s, :], func=AF.Square, accum_out=self.ss[:, s:s + 1]),
                 reads=[self.xinR[b]], writes=[self.xnR, self.ssR])
        P.op("pool", lambda e: e.tensor_scalar(out=self.rstd[:], in0=self.ss[:], scalar1=1.0 / D, scalar2=EPS, op0=ALU.mult, op1=ALU.add),
             reads=[self.ssR], writes=[self.rstdR])
        P.op("pool", lambda e: e.tensor_tensor(out=self.rstd[:], in0=self.rstd[:], in1=C.mhalf[:, 0:2], op=ALU.pow),
             reads=[self.rstdR, C.R_const], writes=[self.rstdR])
        for s in range(2):
            P.op("dve", lambda e, s=s: e.tensor_scalar(out=xn[:, s, :], in0=xin[:, s, :], scalar1=self.rstd[:, s:s + 1], scalar2=None, op0=ALU.mult),
                 reads=[self.xinR[b], self.rstdR], writes=[self.xnR])

    def transpose_mod(self, hb, l, ivec, col, pbis=(0, 1)):
        C, P = self.C, self.C.P
        xn = self.xn
        for kp in range(4):
            pbi = pbis[kp % 2]
            for q in range(2):
                kc = kp * 2 + q
                for s in range(2):
                    P.op("pe", lambda e, kc=kc, s=s, q=q, pbi=pbi: e.transpose(
                        out=C.pb[pbi][:, q * 256 + s * 128:q * 256 + (s + 1) * 128], in_=xn[:, s, kc * 128:(kc + 1) * 128], identity=C.ident[:]),
                        reads=[self.xnR, C.R_const], writes=[C.pbR[pbi]])
            for q in range(2):
                kc = kp * 2 + q
                P.op("act", lambda e, kc=kc, q=q, pbi=pbi: e.activation(
                    out=self.hT[hb][:, kc, :], in_=C.pb[pbi][:, q * 256:(q + 1) * 256], func=AF.Identity,
                    bias=C.modcol[l][:, ivec * 8 + kc, col:col + 1], scale=C.sc1p[l][:, (ivec + 1) * 8 + kc, col:col + 1]),
                    reads=[C.pbR[pbi], C.R_mod], writes=[self.hTR[hb]])


def phase_ffn(C, l, half, src, with_ctx=False, final=False):
    P, nc = C.P, C.nc
    ivec = 0 if half == 0 else 6
    with ExitStack() as es:
        T = lambda n, s, d: C.T(n, s, d, es)
        wgu = T("wgu", [128, 8, 2 * FF], BF16)
        wdn = T("wdn", [128, 22, 1024], BF16)
        wguR = [Reg("wgu%d" % k) for k in range(8)]
        wdnR = [Reg("wdn%d" % k) for k in range(2)]
        fr = Front(C, es, nbuf=3)
        aT = T("aT", [128, 22, TT], BF16)
        aTR = Reg("aT")
        sil = [T("sil%d" % i, [128, TT], F32) for i in range(3)]
        silR = [Reg("sil%d" % i) for i in range(3)]
        ty = [T("ty%d" % i, [128, 512], F32) for i in range(2)]
        tyR = [Reg("ty%d" % i) for i in range(2)]
        gate = T("gate", [128, 1024], F32)
        gateR = Reg("gate")
        gtmp = [T("gtmp%d" % i, [128, 128], F32) for i in range(2)]
        gtmpR = [Reg(), Reg()]
        if with_ctx:
            gatec = T("gatec", [128, 1024], F32)
            gatecR = Reg("gatec")
        if final:
            gfin = T("gfin", [128, 1024], F32)
            gfinR = Reg("gfin")
            ss2 = T("ss2", [128, 2], F32)
            ss2R = Reg("ss2")
            rs2 = T("rs2", [128, 2], F32)
            rs2R = Reg("rs2")
            fjunk = T("fjunk", [128, 1024], BF16)
            fjunkR = Reg("fjunk")

        wg_src = C.w_gu[l, half]
        for kc in range(8):
            P.op("pool", lambda e, kc=kc: e.dma_start(out=wgu[:, kc, :], in_=wg_src[kc * 128:(kc + 1) * 128, :]),
                 writes=[wguR[kc]], dma=True)
        wd_src = C.w_dn[l, half].rearrange("(j p) d -> p j d", p=128)
        for k in range(2):
            P.op("pool", lambda e, k=k: e.dma_start(out=wdn[:, k * 11:(k + 1) * 11, :], in_=wd_src[:, k * 11:(k + 1) * 11, :]),
                 writes=[wdnR[k]], dma=True)

        tiles = []
        if with_ctx:
            tiles.append((C.ctx_in[0:CTX, :], C.cx[0:CTX, :], 1, None, None))
        for t in range(NT):
            tiles.append((src[t * TT:(t + 1) * TT, :], C.out[t * TT:(t + 1) * TT, :], 0, C.outR[t] if src is C.out else None, C.outR[t]))
        ntl = len(tiles)

        build_bc(C, gate, gateR, lambda kc: C.modcol[l][:, (ivec + 2) * 8 + kc, 0:1], 0.5, pbis=(6, 7), tmp=gtmp, tmpR=gtmpR)
        if with_ctx:
            build_bc(C, gatec, gatecR, lambda kc: C.modcol[l][:, (ivec + 2) * 8 + kc, 1:2], 0.5, pbis=(6, 7), tmp=gtmp, tmpR=gtmpR)
        if final:
            P.op("sp", lambda e: e.dma_start(out=gfin[:], in_=C.fng[0:1, :].partition_broadcast(128)), writes=[gfinR], dma=True)

        for i in range(min(3, ntl)):
            fr.load(tiles[i][0], i % 3, tiles[i][3])
        fr.norm(0)
        fr.transpose_mod(0, l, ivec, tiles[0][2])
        def finish(i):
            xb = i % 3
            if final:
                for s in range(2):
                    P.op("act", lambda e, s=s: e.activation(out=fjunk[:], in_=fr.xin[xb][:, s, :], func=AF.Square, accum_out=ss2[:, s:s + 1]),
                         reads=[fr.xinR[xb]], writes=[fjunkR, ss2R], relax=True)
                P.op("pool", lambda e: e.tensor_scalar(out=rs2[:], in0=ss2[:], scalar1=1.0 / D, scalar2=EPS, op0=ALU.mult, op1=ALU.add),
                     reads=[ss2R], writes=[rs2R])
                P.op("pool", lambda e: e.tensor_tensor(out=rs2[:], in0=rs2[:], in1=C.mhalf[:, 0:2], op=ALU.pow),
                     reads=[rs2R, C.R_const], writes=[rs2R])
                for s in range(2):
                    P.op("dve", lambda e, s=s: e.scalar_tensor_tensor(
                        out=fr.xin[xb][:, s, :], in0=fr.xin[xb][:, s, :], scalar=rs2[:, s:s + 1], in1=gfin[:], op0=ALU.mult, op1=ALU.mult),
                        reads=[fr.xinR[xb], rs2R, gfinR], writes=[fr.xinR[xb]])
            dst = tiles[i][1]
            P.op("sp", lambda e: e.dma_start(out=dst.rearrange("(s p) d -> p s d", p=128), in_=fr.xin[xb][:]),
                 reads=[fr.xinR[xb]], writes=[tiles[i][4]] if tiles[i][4] is not None else [], dma=True)
            if i + 3 < ntl:
                fr.load(tiles[i + 3][0], xb, tiles[i + 3][3])

        for i in range(ntl):
            xb = i % 3
            hb = i % 2
            col = tiles[i][2]
            if i + 1 < ntl:
                fr.norm((i + 1) % 3)
            for j in range(22):
                pbi = 2 + (j % 4)
                for part in range(2):
                    for kc in range(8):
                        P.op("pe", lambda e, kc=kc, j=j, part=part, pbi=pbi, hb=hb: e.matmul(
                            C.pb[pbi][:, part * 256:(part + 1) * 256],
                            lhsT=wgu[:, kc, part * FF + j * 128: part * FF + (j + 1) * 128], rhs=fr.hT[hb][:, kc, :],
                            start=(kc == 0), stop=(kc == 7)),
                            reads=[wguR[kc], fr.hTR[hb]], writes=[C.pbR[pbi]])
                sb = j % 3
                P.op("act", lambda e, pbi=pbi, sb=sb: e.activation(out=sil[sb][:], in_=C.pb[pbi][:, 0:256], func=AF.Silu),
                     reads=[C.pbR[pbi]], writes=[silR[sb]])
                P.op("dve", lambda e, pbi=pbi, sb=sb, j=j: e.tensor_tensor(out=aT[:, j, :], in0=sil[sb][:], in1=C.pb[pbi][:, 256:512], op=ALU.mult),
                     reads=[silR[sb], C.pbR[pbi]], writes=[aTR])
            if final and i > 0:
                finish(i - 1)
            if i + 1 < ntl:
                fr.transpose_mod((i + 1) % 2, l, ivec, tiles[i + 1][2])
            gt = gatec if (with_ctx and col == 1) else gate
            gtR = gatecR if (with_ctx and col == 1) else gateR
            for s in range(2):
                for hf in range(2):
                    pbi = 6 + ((s * 2 + hf) % 2)
                    for j in range(22):
                        P.op("pe", lambda e, j=j, s=s, hf=hf, pbi=pbi: e.matmul(
                            C.pb[pbi][:, :], lhsT=aT[:, j, s * 128:(s + 1) * 128], rhs=wdn[:, j, hf * 512:(hf + 1) * 512],
                            start=(j == 0), stop=(j == 21)),
                            reads=[aTR, wdnR[j // 11]], writes=[C.pbR[pbi]])
                    tb = (s * 2 + hf) % 2
                    P.op("dve", lambda e, pbi=pbi, tb=tb, hf=hf, gt=gt: e.tensor_tensor(
                        out=ty[tb][:], in0=C.pb[pbi][:, :], in1=gt[:, hf * 512:(hf + 1) * 512], op=ALU.mult),
                        reads=[C.pbR[pbi], gtR], writes=[tyR[tb]])
                    P.op("pool", lambda e, tb=tb, s=s, hf=hf, xb=xb: e.tensor_tensor(
                        out=fr.xin[xb][:, s, hf * 512:(hf + 1) * 512], in0=fr.xin[xb][:, s, hf * 512:(hf + 1) * 512], in1=ty[tb][:], op=ALU.add),
                        reads=[tyR[tb], fr.xinR[xb]], writes=[fr.xinR[xb]])
            if not final:
                finish(i)
        if final:
            finish(ntl - 1)
        P.barrier()
        P.emit()


def phase_mix0(C):
    P, nc = C.P, C.nc
    l = 0
    LAM_INIT = 0.8 - 0.6 * math.exp(-0.3 * 0)
    with ExitStack() as es_kv:
        Tk = lambda n, s, d: C.T(n, s, d, es_kv)
        KT = Tk("KT", [128, 4, N + CTX], BF16)
        V = Tk("V", [128, NKC, 4, 130], BF16)
        KTR = [Reg("KT%d" % i) for i in range(NT + 1)]
        VR = [Reg("V%d" % i) for i in range(NT + 1)]
        VoneR = Reg("Vone")
        with ExitStack() as es:
            T = lambda n, s, d: C.T(n, s, d, es)
            w_in = T("w_in", [128, 8, 2560], BF16)
            w_rot = T("w_rot", [128, 8, 1024], BF16)
            wR = [Reg("w_in%d" % k) for k in range(8)]
            wrR = Reg("w_rot")
            fr = Front(C, es, nbuf=2)
            cs = [T("cs%d" % i, [128, 2, TT], F32) for i in range(2)]
            csR = [Reg(), Reg()]
            t1 = [T("t1_%d" % i, [128, TT], F32) for i in range(2)]
            t2 = [T("t2_%d" % i, [128, TT], F32) for i in range(2)]
            t1R = [Reg(), Reg()]
            t2R = [Reg(), Reg()]
            qT = [T("qT%d" % i, [128, 4, TT], BF16) for i in range(2)]
            qTR = [Reg(), Reg()]
            uT = T("uT", [128, 4, TT], F32)
            uTR = Reg("uT")
            vg = [T("vg%d" % i, [128, 512], F32) for i in range(2)]
            vgR = [Reg(), Reg()]
            vgc = [T("vgc%d" % i, [128, 512], F32) for i in range(2)]
            vgcR = [Reg(), Reg()]
            vgn = [T("vgn%d" % i, [128, 512], BF16) for i in range(2)]
            vgnR = [Reg(), Reg()]
            st2 = [T("st%d" % i, [128, 4, 2], F32) for i in range(2)]
            mv2 = [T("mv%d" % i, [128, 4, 2], F32) for i in range(2)]
            rs42 = [T("rs4%d" % i, [128, 4], F32) for i in range(2)]
            st2R, mv2R, rs42R = [Reg(), Reg()], [Reg(), Reg()], [Reg(), Reg()]
            sgT = [T("sgT%d" % i, [128, 4, TT], BF16) for i in range(2)]
            sgTR = [Reg(), Reg()]
            lng = T("lng", [128, 512], F32)
            lnb = T("lnb", [128, 512], F32)
            wsT = T("wsT", [128, 4, 128], BF16)
            bsr = T("bsr", [1, 512], BF16)
            onr = T("onr", [1, 128], BF16)
            cR = Reg("m1const")

            w_src = C.w_in.rearrange("(kc p) n -> p kc n", p=128)
            for kc in range(8):
                for hh in range(2):
                    P.op("pool", lambda e, kc=kc, hh=hh: e.dma_start(out=w_in[:, kc, hh * 1280:(hh + 1) * 1280],
                                                                  in_=C.w_in[kc * 128:(kc + 1) * 128, hh * 1280:(hh + 1) * 1280]),
                         writes=[wR[kc]], dma=True)
            wr_src = C.w_rot.rearrange("(kc p) n -> p kc n", p=128)
            for hh in range(2):
                P.op("pool", lambda e, hh=hh: e.dma_start(out=w_rot[:, hh * 4:(hh + 1) * 4, :], in_=wr_src[:, hh * 4:(hh + 1) * 4, :]),
                     writes=[wrR], dma=True)
            P.op("pool", lambda e: e.dma_start(out=wsT[:], in_=C.sgu_wT.rearrange("g q p -> q g p")), writes=[cR], dma=True)
            P.op("pool", lambda e: e.dma_start(out=bsr[:], in_=C.sgu_b[0:1, :]), writes=[cR], dma=True)
            P.op("sp", lambda e: e.dma_start(out=lng[:], in_=C.sgu_g[0:1, :].partition_broadcast(128)), writes=[cR], dma=True)
            P.op("sp", lambda e: e.dma_start(out=lnb[:], in_=C.sgu_bn[0:1, :].partition_broadcast(128)), writes=[cR], dma=True)
            P.op("dve", lambda e: e.memset(onr[:], 1.0), writes=[cR])
            P.op("pool", lambda e: e.memset(V[:, :, :, 128:130], 1.0), writes=[VoneR])

            tiles = [(C.cx[0:CTX, :], 1, NT)] + [(C.out[t * TT:(t + 1) * TT, :], 0, t) for t in range(NT)]
            ntl = len(tiles)
            rot = [0]

            def nextpb():
                rot[0] += 1
                return 2 + (rot[0] % 4)

            fr.load(tiles[0][0], 0)
            fr.load(tiles[1][0], 1)
            fr.norm(0)
            fr.transpose_mod(0, l, 3, tiles[0][1])
            for i in range(ntl):
                src, col, tix = tiles[i]
                hb = i % 2
                hT = fr.hT[hb]
                hTR = fr.hTR[hb]
                is_ctx = (col == 1)
                tok0 = N if is_ctx else tix * TT
                if i + 1 < ntl:
                    fr.norm((i + 1) % 2)
                if i + 2 < ntl:
                    fr.load(tiles[i + 2][0], i % 2)
                cb = i % 2
                if not is_ctx:
                    P.op("sp", lambda e, cb=cb, tok0=tok0: e.dma_start(out=cs[cb][:, 0, :], in_=C.rope_c[:, tok0:tok0 + TT]), writes=[csR[cb]], dma=True)
                    P.op("sp", lambda e, cb=cb, tok0=tok0: e.dma_start(out=cs[cb][:, 1, :], in_=C.rope_s[:, tok0:tok0 + TT]), writes=[csR[cb]], dma=True)
                def tile_body(hT, hTR, is_ctx, tok0, tix, cb):
                    def do_qk(t_, h):
                        pbi = nextpb()
                        c0 = t_ * 512 + h * 128
                        for kc in range(8):
                            P.op("pe", lambda e, kc=kc, c0=c0, pbi=pbi: e.matmul(
                                C.pb[pbi][:, 0:TT], lhsT=w_in[:, kc, c0:c0 + 128], rhs=hT[:, kc, :], start=(kc == 0), stop=(kc == 7)),
                                reads=[wR[kc], hTR], writes=[C.pbR[pbi]])
                        if is_ctx:
                            P.op("act", lambda e, pbi=pbi, h=h: e.activation(out=KT[:, h, tok0:tok0 + TT], in_=C.pb[pbi][:, 0:TT], func=AF.Identity),
                                 reads=[C.pbR[pbi]], writes=[KTR[tix]])
                            return
                        for kc in range(8):
                            P.op("pe", lambda e, kc=kc, c0=c0, pbi=pbi: e.matmul(
                                C.pb[pbi][:, TT:2 * TT], lhsT=w_rot[:, kc, c0:c0 + 128], rhs=hT[:, kc, :], start=(kc == 0), stop=(kc == 7)),
                                reads=[wrR, hTR], writes=[C.pbR[pbi]])
                        tb = h % 2
                        P.op("dve", lambda e, pbi=pbi, tb=tb: e.tensor_tensor(out=t1[tb][:], in0=C.pb[pbi][:, 0:TT], in1=cs[cb][:, 0, :], op=ALU.mult),
                             reads=[C.pbR[pbi], csR[cb]], writes=[t1R[tb]])
                        P.op("dve", lambda e, pbi=pbi, tb=tb: e.tensor_tensor(out=t2[tb][:], in0=C.pb[pbi][:, TT:2 * TT], in1=cs[cb][:, 1, :], op=ALU.mult),
                             reads=[C.pbR[pbi], csR[cb]], writes=[t2R[tb]])
                        if t_ == 1:
                            P.op("pool", lambda e, tb=tb, h=h: e.tensor_tensor(out=KT[:, h, tok0:tok0 + TT], in0=t1[tb][:], in1=t2[tb][:], op=ALU.add),
                                 reads=[t1R[tb], t2R[tb]], writes=[KTR[tix]])
                        else:
                            P.op("pool", lambda e, tb=tb, h=h: e.tensor_tensor(out=qT[cb][:, h, :], in0=t1[tb][:], in1=t2[tb][:], op=ALU.add),
                                 reads=[t1R[tb], t2R[tb]], writes=[qTR[cb]])

                    def do_v(s):
                        pbi = nextpb()
                        chunk = tok0 // 128 + s
                        for kc in range(8):
                            P.op("pe", lambda e, kc=kc, s=s, pbi=pbi: e.matmul(
                                C.pb[pbi][:, :], lhsT=hT[:, kc, s * 128:(s + 1) * 128], rhs=w_in[:, kc, 1024:1536], start=(kc == 0), stop=(kc == 7)),
                                reads=[wR[kc], hTR], writes=[C.pbR[pbi]])
                        P.op("act", lambda e, pbi=pbi, chunk=chunk: e.activation(
                            out=V[:, chunk, :, 0:128], in_=C.pb[pbi][:, :].rearrange("p (h e) -> p h e", h=4), func=AF.Identity),
                            reads=[C.pbR[pbi]], writes=[VR[tix]])

                    def do_u(g):
                        pbi = nextpb()
                        c0 = 1536 + g * 128
                        for kc in range(8):
                            P.op("pe", lambda e, kc=kc, c0=c0, pbi=pbi: e.matmul(
                                C.pb[pbi][:, 0:TT], lhsT=w_in[:, kc, c0:c0 + 128], rhs=hT[:, kc, :], start=(kc == 0), stop=(kc == 7)),
                                reads=[wR[kc], hTR], writes=[C.pbR[pbi]])
                        P.op("act", lambda e, pbi=pbi, g=g: e.activation(out=uT[:, g, :], in_=C.pb[pbi][:, 0:TT], func=AF.Gelu_apprx_tanh),
                             reads=[C.pbR[pbi]], writes=[uTR])

                    def do_vg(s):
                        pbi = nextpb()
                        for kc in range(8):
                            P.op("pe", lambda e, kc=kc, s=s, pbi=pbi: e.matmul(
                                C.pb[pbi][:, :], lhsT=hT[:, kc, s * 128:(s + 1) * 128], rhs=w_in[:, kc, 2048:2560], start=(kc == 0), stop=(kc == 7)),
                                reads=[wR[kc], hTR], writes=[C.pbR[pbi]])
                        P.op("act", lambda e, pbi=pbi, s=s: e.activation(out=vg[s][:], in_=C.pb[pbi][:, :], func=AF.Gelu_apprx_tanh),
                             reads=[C.pbR[pbi]], writes=[vgR[s]])
                        for g in range(4):
                            P.op("act", lambda e, g=g, s=s: e.activation(out=vgc[s][:, g * 128:(g + 1) * 128], in_=vg[s][:, g * 128:(g + 1) * 128],
                                                                       func=AF.Identity, accum_out=st2[s][:, g, 0:1]),
                                 reads=[vgR[s]], writes=[st2R[s], vgcR[s]])
                            P.op("act", lambda e, g=g, s=s: e.activation(out=vgc[s][:, g * 128:(g + 1) * 128], in_=vg[s][:, g * 128:(g + 1) * 128],
                                                                       func=AF.Square, accum_out=st2[s][:, g, 1:2]),
                                 reads=[vgR[s]], writes=[st2R[s], vgcR[s]])

                    def do_ln_small(s):
                        st, mv, rs4 = st2[s], mv2[s], rs42[s]
                        P.op("pool", lambda e: e.tensor_scalar(out=mv[:, :, 0], in0=st[:, :, 0], scalar1=1.0 / 128, scalar2=None, op0=ALU.mult),
                             reads=[st2R[s]], writes=[mv2R[s]])
                        P.op("pool", lambda e: e.tensor_tensor(out=mv[:, :, 1], in0=mv[:, :, 0], in1=mv[:, :, 0], op=ALU.mult),
                             reads=[mv2R[s]], writes=[mv2R[s]])
                        P.op("pool", lambda e: e.tensor_scalar(out=rs4[:], in0=st[:, :, 1], scalar1=1.0 / 128, scalar2=EPS, op0=ALU.mult, op1=ALU.add),
                             reads=[st2R[s]], writes=[rs42R[s]])
                        P.op("pool", lambda e: e.tensor_tensor(out=rs4[:], in0=rs4[:], in1=mv[:, :, 1], op=ALU.subtract),
                             reads=[rs42R[s], mv2R[s]], writes=[rs42R[s]])
                        P.op("pool", lambda e: e.tensor_tensor(out=rs4[:], in0=rs4[:], in1=C.mhalf[:, 0:4], op=ALU.pow),
                             reads=[rs42R[s], C.R_const], writes=[rs42R[s]])

                    def do_ln_big(s):
                        mv, rs4 = mv2[s], rs42[s]
                        for g in range(4):
                            P.op("dve", lambda e, g=g, s=s: e.tensor_scalar(
                                out=vgc[s][:, g * 128:(g + 1) * 128], in0=vg[s][:, g * 128:(g + 1) * 128],
                                scalar1=mv[:, g, 0:1], scalar2=rs4[:, g:g + 1], op0=ALU.subtract, op1=ALU.mult),
                                reads=[vgR[s], mv2R[s], rs42R[s]], writes=[vgcR[s]])
                        P.op("pool", lambda e, s=s: e.tensor_tensor(out=vgc[s][:], in0=vgc[s][:], in1=lng[:], op=ALU.mult),
                             reads=[vgcR[s], cR], writes=[vgcR[s]])
                        P.op("pool", lambda e, s=s: e.tensor_tensor(out=vgn[s][:], in0=vgc[s][:], in1=lnb[:], op=ALU.add),
                             reads=[vgcR[s], cR], writes=[vgnR[s]])

                    def do_mix(s):
                        pbm = 6 + (s % 2)
                        for g in range(4):
                            P.op("pe", lambda e, g=g, s=s, pbm=pbm: e.matmul(
                                C.pb[pbm][:, g * 128:(g + 1) * 128], lhsT=vgn[s][:, g * 128:(g + 1) * 128], rhs=wsT[:, g, :], start=True, stop=False),
                                reads=[vgnR[s], cR], writes=[C.pbR[pbm]])
                            P.op("pe", lambda e, g=g, pbm=pbm: e.matmul(
                                C.pb[pbm][:, g * 128:(g + 1) * 128], lhsT=onr[0:1, :], rhs=bsr[0:1, g * 128:(g + 1) * 128], start=False, stop=True),
                                reads=[cR], writes=[C.pbR[pbm]])
                        P.op("dve", lambda e, s=s, pbm=pbm: e.tensor_tensor(
                            out=sgT[cb][:, :, s * 128:(s + 1) * 128], in0=C.pb[pbm][:, :].rearrange("p (g q) -> p g q", g=4),
                            in1=uT[:, :, s * 128:(s + 1) * 128], op=ALU.mult),
                            reads=[C.pbR[pbm], uTR], writes=[sgTR[cb]])

                    if is_ctx:
                        for h in range(4):
                            do_qk(1, h)
                        for s in range(2):
                            do_v(s)
                    else:
                        do_vg(0)
                        do_vg(1)
                        for h in range(4):
                            do_qk(0, h)
                        P.op("sp", lambda e, cb=cb, tok0=tok0: e.dma_start(out=C.QTs[:, :, tok0:tok0 + TT], in_=qT[cb][:]), reads=[qTR[cb]], dma=True)
                        do_ln_small(0)
                        do_ln_small(1)
                        do_ln_big(0)
                        for h in range(4):
                            do_qk(1, h)
                        do_ln_big(1)
                        for s in range(2):
                            do_v(s)
                        for g in range(4):
                            do_u(g)
                        do_mix(0)
                        do_mix(1)
                        P.op("sp", lambda e, cb=cb, tok0=tok0: e.dma_start(out=C.SGs[:, :, tok0:tok0 + TT], in_=sgT[cb][:]), reads=[sgTR[cb]], dma=True)

                tile_body(hT, hTR, is_ctx, tok0, tix, cb)
                if i + 1 < ntl:
                    fr.transpose_mod((i + 1) % 2, l, 3, tiles[i + 1][1])
            P.barrier()
            P.emit()

        import os
        if os.environ.get('SKIP_M2'):
            return
        with ExitStack() as es:
            T = lambda n, s, d: C.T(n, s, d, es)
            w_out = T("w_out", [128, 8, 1024], BF16)
            woR = Reg("w_out")
            gate = T("gate5", [128, 1024], F32)
            gateR = Reg()
            gtmp = [T("gtmp%d" % i, [128, 128], F32) for i in range(2)]
            gtmpR = [Reg(), Reg()]
            gsub = T("gsub", [128, 128], F32)
            lamt = T("lamt", [128, 256], F32)
            lj = T("lj", [128, 64], F32)
            lsc = T("lsc", [128, 8], F32)
            cR = Reg("m2const")
            qz = [[T("qz%d_%d" % (m, i), [128, 4, TT], BF16) for i in range(2)] for m in range(2)]
            qTR = [Reg(), Reg()]
            for m in range(2):
                for i in range(2):
                    P.op("pool", lambda e, m=m, i=i: e.memset(qz[m][i][:], 0.0), writes=[qTR[i]])
            catT = [T("catT%d" % i, [128, 8, TT], BF16) for i in range(2)]
            catAR = [Reg(), Reg()]
            catSR = [Reg(), Reg()]
            xres = [T("xres%d" % i, [128, 2, 1024], F32) for i in range(2)]
            xresR = [Reg(), Reg()]
            pT = [T("pT%d" % i, [128, 2 * TT], BF16) for i in range(2)]
            pTR = [Reg(), Reg()]
            ty = [T("ty%d" % i, [128, 512], F32) for i in range(2)]
            tyR = [Reg(), Reg()]

            wo_src = C.w_out.rearrange("(c p) d -> p c d", p=128)
            for hh in range(2):
                P.op("pool", lambda e, hh=hh: e.dma_start(out=w_out[:, hh * 4:(hh + 1) * 4, :], in_=wo_src[:, hh * 4:(hh + 1) * 4, :]),
                     writes=[woR], dma=True)
            P.op("sp", lambda e: e.dma_start(out=gsub[:], in_=C.subg[0:1, :].partition_broadcast(128)), writes=[cR], dma=True)
            P.op("sp", lambda e: e.dma_start(out=lamt[:], in_=C.lam_in[0:1, :].partition_broadcast(128)), writes=[cR], dma=True)
            P.op("dve", lambda e: e.tensor_scalar(out=gsub[:], in0=gsub[:], scalar1=float(1.0 - LAM_INIT), scalar2=None, op0=ALU.mult),
                 reads=[cR], writes=[cR])
            P.op("dve", lambda e: e.tensor_tensor(out=lj[:], in0=lamt[:, 0:64], in1=lamt[:, 64:128], op=ALU.mult), reads=[cR], writes=[cR])
            P.op("act", lambda e: e.activation(out=lj[:], in_=lj[:], func=AF.Identity, accum_out=lsc[:, 0:1]), reads=[cR], writes=[cR])
            P.op("dve", lambda e: e.tensor_tensor(out=lj[:], in0=lamt[:, 128:192], in1=lamt[:, 192:256], op=ALU.mult), reads=[cR], writes=[cR])
            P.op("act", lambda e: e.activation(out=lj[:], in_=lj[:], func=AF.Identity, accum_out=lsc[:, 1:2]), reads=[cR], writes=[cR])
            P.op("act", lambda e: e.activation(out=lsc[:, 2:4], in_=lsc[:, 0:2], func=AF.Exp), reads=[cR], writes=[cR])
            P.op("dve", lambda e: e.tensor_tensor(out=lsc[:, 4:5], in0=lsc[:, 2:3], in1=lsc[:, 3:4], op=ALU.subtract), reads=[cR], writes=[cR])
            P.op("dve", lambda e: e.tensor_scalar(out=lsc[:, 5:6], in0=lsc[:, 4:5], scalar1=-1.0, scalar2=float(-LAM_INIT), op0=ALU.mult, op1=ALU.add),
                 reads=[cR], writes=[cR])
            build_bc(C, gate, gateR, lambda kc: C.modcol[l][:, 5 * 8 + kc, 0:1], 1.0, pbis=(2, 3), tmp=gtmp, tmpR=gtmpR)

            def loads(t):
                b = t % 2
                tok0 = t * TT
                for m in range(2):
                    P.op("sp", lambda e, m=m: e.dma_start(out=qz[m][b][m * 64:(m + 1) * 64, :, :], in_=C.QTs[m * 64:(m + 1) * 64, :, tok0:tok0 + TT]),
                         writes=[qTR[b]], dma=True)
                P.op("sp", lambda e: e.dma_start(out=catT[b][:, 4:8, :], in_=C.SGs[:, :, tok0:tok0 + TT]), writes=[catSR[b]], dma=True)
                P.op("sp", lambda e: e.dma_start(out=xres[b][:], in_=C.out[tok0:tok0 + TT, :].rearrange("(s p) d -> p s d", p=128)),
                     reads=[C.outR[t]], writes=[xresR[b]], dma=True)

            steps = [(h, kc) for h in range(4) for kc in range(NKC)]
            nst = len(steps)
            rr = [T("rr%d" % i, [128, 4], F32) for i in range(2)]
            rrR = [[Reg() for _ in range(3)] for _ in range(2)]
            ot1 = [T("ot1_%d" % i, [128, 128], F32) for i in range(2)]
            ot1R = [Reg(), Reg()]
            oo = [T("oo%d" % i, [128, 128], F32) for i in range(2)]
            ooR = [Reg(), Reg()]
            oj = [T("oj%d" % i, [128, 128], F32) for i in range(2)]
            ojR = [Reg(), Reg()]
            oss = [T("oss%d" % i, [128, 2], F32) for i in range(2)]
            ossR = [[Reg(), Reg()], [Reg(), Reg()]]
            attn = [T("attn%d" % i, [128, 2, 512], F32) for i in range(2)]
            attnR = [Reg(), Reg()]

            def epilogue_a(t, h):
                for qb in range(2):
                    p0, p1 = 2 + (h % 2) * 2, 3 + (h % 2) * 2
                    o0 = qb * 130
                    r_, rR_ = rr[qb], rrR[qb]
                    P.op("dve", lambda e, p0=p0, o0=o0, r_=r_: e.reciprocal(out=r_[:, 0:1], in_=C.pb[p0][:, o0 + 128:o0 + 129]), reads=[C.pbR[p0]], writes=[rR_[0]])
                    P.op("dve", lambda e, p1=p1, o0=o0, r_=r_: e.reciprocal(out=r_[:, 1:2], in_=C.pb[p1][:, o0 + 128:o0 + 129]), reads=[C.pbR[p1]], writes=[rR_[1]])
                    P.op("dve", lambda e, r_=r_: e.tensor_tensor(out=r_[:, 2:3], in0=r_[:, 1:2], in1=lsc[:, 5:6], op=ALU.mult), reads=[rR_[1], cR], writes=[rR_[2]])
                    P.op("dve", lambda e, p1=p1, o0=o0, r_=r_, qb=qb: e.tensor_scalar(out=ot1[qb][:], in0=C.pb[p1][:, o0:o0 + 128], scalar1=r_[:, 2:3], scalar2=None, op0=ALU.mult),
                         reads=[C.pbR[p1], rR_[2]], writes=[ot1R[qb]])
                    P.op("dve", lambda e, p0=p0, o0=o0, r_=r_, qb=qb: e.scalar_tensor_tensor(out=oo[qb][:], in0=C.pb[p0][:, o0:o0 + 128], scalar=r_[:, 0:1], in1=ot1[qb][:],
                                                                                op0=ALU.mult, op1=ALU.add),
                         reads=[C.pbR[p0], rR_[0], ot1R[qb]], writes=[ooR[qb]])

            def epilogue_b(t, h):
                b = t % 2
                for qb in range(2):
                    P.op("act", lambda e, qb=qb: e.activation(out=oj[qb][:], in_=oo[qb][:], func=AF.Square, accum_out=oss[qb][:, 0:1]), reads=[ooR[qb]], writes=[ojR[qb], ossR[qb][0]])
                for qb in range(2):
                    P.op("pool", lambda e, qb=qb: e.tensor_scalar(out=oss[qb][:, 1:2], in0=oss[qb][:, 0:1], scalar1=1.0 / 128, scalar2=EPS, op0=ALU.mult, op1=ALU.add),
                         reads=[ossR[qb][0]], writes=[ossR[qb][1]])
                    P.op("pool", lambda e, qb=qb: e.tensor_tensor(out=oss[qb][:, 1:2], in0=oss[qb][:, 1:2], in1=C.mhalf[:, 0:1], op=ALU.pow),
                         reads=[ossR[qb][1], C.R_const], writes=[ossR[qb][1]])
                for qb in range(2):
                    P.op("dve", lambda e, qb=qb, h=h, b=b: e.scalar_tensor_tensor(out=attn[b][:, qb, h * 128:(h + 1) * 128], in0=oo[qb][:], scalar=oss[qb][:, 1:2], in1=gsub[:],
                                                                              op0=ALU.mult, op1=ALU.mult),
                         reads=[ooR[qb], ossR[qb][1], cR], writes=[attnR[b]])

            def tail(t):
                b = t % 2
                for qb in range(2):
                    for h in range(4):
                        P.op("pe", lambda e, qb=qb, h=h: e.transpose(out=C.pb[6 + qb][:, h * 128:(h + 1) * 128], in_=attn[b][:, qb, h * 128:(h + 1) * 128], identity=C.ident[:]),
                             reads=[attnR[b], C.R_const], writes=[C.pbR[6 + qb]])
                    P.op("dve", lambda e, qb=qb: e.tensor_copy(out=catT[b][:, 0:4, qb * 128:(qb + 1) * 128],
                                                             in_=C.pb[6 + qb][:, :].rearrange("p (h q) -> p h q", h=4)),
                         reads=[C.pbR[6 + qb]], writes=[catAR[b]])
                for qb in range(2):
                    for hf in range(2):
                        pbi = 6 + ((qb * 2 + hf) % 2)
                        for c in range(8):
                            P.op("pe", lambda e, c=c, qb=qb, hf=hf, pbi=pbi: e.matmul(
                                C.pb[pbi][:, :], lhsT=catT[b][:, c, qb * 128:(qb + 1) * 128], rhs=w_out[:, c, hf * 512:(hf + 1) * 512],
                                start=(c == 0), stop=(c == 7)),
                                reads=[catAR[b], catSR[b], woR], writes=[C.pbR[pbi]])
                        tb = (qb * 2 + hf) % 2
                        P.op("dve", lambda e, pbi=pbi, tb=tb, hf=hf: e.tensor_tensor(out=ty[tb][:], in0=C.pb[pbi][:, :], in1=gate[:, hf * 512:(hf + 1) * 512], op=ALU.mult),
                             reads=[C.pbR[pbi], gateR], writes=[tyR[tb]])
                        P.op("pool", lambda e, tb=tb, qb=qb, hf=hf: e.tensor_tensor(
                            out=xres[b][:, qb, hf * 512:(hf + 1) * 512], in0=xres[b][:, qb, hf * 512:(hf + 1) * 512], in1=ty[tb][:], op=ALU.add),
                            reads=[tyR[tb], xresR[b]], writes=[xresR[b]])
                P.op("sp", lambda e: e.dma_start(out=C.out[t * TT:(t + 1) * TT, :].rearrange("(s p) d -> p s d", p=128), in_=xres[b][:]),
                     reads=[xresR[b]], writes=[C.outR[t]], dma=True)

            def S(t, idx):
                b = t % 2
                h, kc = steps[idx]
                kp = idx % 2
                ktile = NT if kc >= 32 else kc // 2
                for m in range(2):
                    P.op("pe", lambda e, m=m, h=h, kc=kc, kp=kp: e.matmul(
                        C.pb[kp][:, m * TT:(m + 1) * TT], lhsT=KT[:, h, kc * 128:(kc + 1) * 128],
                        rhs=qz[m][b][:, h, :], start=True, stop=True),
                        reads=[KTR[ktile], qTR[b]], writes=[C.pbR[kp]])

            loads(0)
            loads(1)
            S(0, 0)
            for t in range(NT):
                b = t % 2
                for idx in range(nst):
                    h, kc = steps[idx]
                    kp = idx % 2
                    ktile = NT if kc >= 32 else kc // 2
                    if idx + 1 < nst:
                        S(t, idx + 1)
                    elif t + 1 < NT:
                        S(t + 1, 0)
                    P.op("act", lambda e, kp=kp: e.activation(out=pT[kp][:], in_=C.pb[kp][:, :], func=AF.Exp, bias=0.0, scale=0.125),
                         reads=[C.pbR[kp]], writes=[pTR[kp]], relax=True)
                    for m in range(2):
                        po = 2 + (h % 2) * 2 + m
                        for qb in range(2):
                            P.op("pe", lambda e, m=m, kp=kp, qb=qb, po=po, kc=kc, h=h: e.matmul(
                                C.pb[po][:, qb * 130:(qb + 1) * 130], lhsT=pT[kp][:, m * TT + qb * 128:m * TT + (qb + 1) * 128], rhs=V[:, kc, h, 0:130],
                                start=(kc == 0 and qb == 0), stop=(kc == NKC - 1), skip_group_check=True),
                                reads=[pTR[kp], VR[ktile], VoneR], writes=[C.pbR[po]])
                    if idx == 12 and t > 0:
                        tail(t - 1)
                        if t + 1 < NT:
                            loads(t + 1)
                    if kc == NKC - 1:
                        epilogue_a(t, h)
                    if idx % NKC == 6:
                        if idx >= NKC:
                            epilogue_b(t, idx // NKC - 1)
                        elif t > 0:
                            epilogue_b(t - 1, 3)
            epilogue_b(NT - 1, 3)
            tail(NT - 1)
            P.barrier()
            P.emit()


def phase_fourier(C):
    P, nc = C.P, C.nc
    l = 1
    with ExitStack() as es_o:
        hTM = C.T("hTM", [128, 32, 1024], BF16, es_o)
        hR = [Reg("hTM%d" % t) for t in range(NT)]
        with ExitStack() as es:
            T = lambda n, s, d: C.T(n, s, d, es)
            scb = T("scb", [128, 1024], F32)
            shb = T("shb", [128, 1024], F32)
            scbR, shbR = Reg(), Reg()
            gtmp = [T("gtmp%d" % i, [128, 128], F32) for i in range(2)]
            gtmpR = [Reg(), Reg()]
            xin = [T("fx%d" % i, [128, 2, 1024], F32) for i in range(2)]
            xinR = [Reg(), Reg()]
            xn2 = [T("fxn%d" % i, [128, 2, 1024], F32) for i in range(2)]
            xn2R = [Reg(), Reg()]
            sqj = T("sqj", [128, 1024], F32)
            sqjR = Reg()
            ss2 = [T("fss%d" % i, [128, 2], F32) for i in range(2)]
            rs2 = [T("frs%d" % i, [128, 2], F32) for i in range(2)]
            ss2R, rs2R = [Reg(), Reg()], [Reg(), Reg()]
            build_bc(C, shb, shbR, lambda kc: C.modcol[l][:, 3 * 8 + kc, 0:1], 1.0, pbis=(0, 1), tmp=gtmp, tmpR=gtmpR)
            build_bc(C, scb, scbR, lambda kc: C.sc1p[l][:, 4 * 8 + kc, 0:1], 1.0, pbis=(2, 3), tmp=gtmp, tmpR=gtmpR)

            def ld(t):
                P.op("sp", lambda e: e.dma_start(out=xin[t % 2][:], in_=C.out[t * TT:(t + 1) * TT, :].rearrange("(s p) d -> p s d", p=128)),
                     reads=[C.outR[t]], writes=[xinR[t % 2]], dma=True)

            ld(0)
            for t in range(NT):
                b = t % 2
                if t + 1 < NT:
                    ld(t + 1)
                xn, xnR, ss, ssR, rs, rsR = xn2[b], xn2R[b], ss2[b], ss2R[b], rs2[b], rs2R[b]
                for s in range(2):
                    P.op("act", lambda e, s=s, b=b, ss=ss: e.activation(out=sqj[:], in_=xin[b][:, s, :], func=AF.Square, accum_out=ss[:, s:s + 1]),
                         reads=[xinR[b]], writes=[sqjR, ssR], relax=True)
                P.op("pool", lambda e, rs=rs, ss=ss: e.tensor_scalar(out=rs[:], in0=ss[:], scalar1=1.0 / D, scalar2=EPS, op0=ALU.mult, op1=ALU.add),
                     reads=[ssR], writes=[rsR])
                P.op("pool", lambda e, rs=rs: e.tensor_tensor(out=rs[:], in0=rs[:], in1=C.mhalf[:, 0:2], op=ALU.pow), reads=[rsR, C.R_const], writes=[rsR])
                for s in range(2):
                    P.op("dve", lambda e, s=s, b=b, xn=xn, rs=rs: e.scalar_tensor_tensor(out=xn[:, s, :], in0=xin[b][:, s, :], scalar=rs[:, s:s + 1], in1=scb[:],
                                                                           op0=ALU.mult, op1=ALU.mult),
                         reads=[xinR[b], rsR, scbR], writes=[xnR])
                P.op("pool", lambda e, t=t, xn=xn: e.tensor_tensor(out=hTM[:, t * 2 + 0, :], in0=xn[:, 0, :], in1=shb[:], op=ALU.add),
                     reads=[xnR, shbR], writes=[hR[t]])
                P.op("dve", lambda e, t=t, xn=xn: e.tensor_tensor(out=hTM[:, t * 2 + 1, :], in0=xn[:, 1, :], in1=shb[:], op=ALU.add),
                     reads=[xnR, shbR], writes=[hR[t]])
            P.barrier()
            P.emit()
        with ExitStack() as es:
            T = lambda n, s, d: C.T(n, s, d, es)
            NKT = NT // 2
            Ct = [T("Ct%d" % i, [128, 32, TT], BF16) for i in range(2)]
            St = [T("St%d" % i, [128, 32, TT], BF16) for i in range(2)]
            CtR = [Reg(), Reg()]
            StR = [Reg(), Reg()]
            T0 = T("T0", [128, 32, 2], BF16)
            ZT = T("ZT", [128, 16, TT], BF16)
            ZTR = [Reg("ZT%d" % i) for i in range(16)]
            fT = [T("fT%d" % i, [128, 8, TT], BF16) for i in range(2)]
            fTR = [Reg("fT0"), Reg("fT1")]
            wfo = T("wfo", [128, 8, 1024], BF16)
            wfoR = Reg()
            ccC = T("ccC", [128, 2, 256], BF16)
            ccS = T("ccS", [128, 2, 256], BF16)
            ccSn = T("ccSn", [128, 2, 256], BF16)
            cR = Reg()
            gate = T("gate5", [128, 1024], F32)
            gateR = Reg()
            gb = T("gb", [128, 1024], F32)
            gbR = Reg()
            gtmp = [T("gtmp%d" % i, [128, 128], F32) for i in range(2)]
            gtmpR = [Reg(), Reg()]
            xres = [T("xres%d" % i, [128, 2, 1024], F32) for i in range(3)]
            xresR = [Reg(), Reg(), Reg()]
            ty = [T("ty%d" % i, [128, 512], F32) for i in range(2)]
            tyR = [Reg(), Reg()]
            x0 = T("x0", [1, 1024], F32)
            x0R = Reg()
            Z0 = T("Z0", [128, 8, 2], BF16)
            Z0R = Reg()
            f0 = T("f0", [128, 8, 2], BF16)
            f0R = Reg()

            wf_src = C.w_fo.rearrange("(c p) d -> p c d", p=128)
            for hh in range(2):
                P.op("pool", lambda e, hh=hh: e.dma_start(out=wfo[:, hh * 4:(hh + 1) * 4, :], in_=wf_src[:, hh * 4:(hh + 1) * 4, :]),
                     writes=[wfoR], dma=True)
            P.op("sp", lambda e: e.dma_start(out=ccC[:], in_=C.ccC[:, :, :]), writes=[cR], dma=True)
            P.op("sp", lambda e: e.dma_start(out=ccS[:], in_=C.ccS[:, :, :]), writes=[cR], dma=True)
            P.op("sp", lambda e: e.dma_start(out=ccSn[:], in_=C.ccSn[:, :, :]), writes=[cR], dma=True)
            P.op("sp", lambda e: e.dma_start(out=T0[:], in_=C.dft0[:, :, :]), writes=[cR], dma=True)
            P.op("sp", lambda e: e.dma_start(out=x0[:], in_=C.out[0:1, :]), writes=[x0R], dma=True)
            P.op("sp", lambda e: e.dma_start(out=gb[:], in_=C.b_fo[0:1, :].partition_broadcast(128)), writes=[gbR], dma=True)
            build_bc(C, gate, gateR, lambda kc: C.modcol[l][:, 5 * 8 + kc, 0:1], 1.0, pbis=(6, 7), tmp=gtmp, tmpR=gtmpR)
            P.op("pool", lambda e: e.tensor_tensor(out=gb[:], in0=gb[:], in1=gate[:], op=ALU.mult), reads=[gbR, gateR], writes=[gbR])

            def rows(q):
                kt, v = q // 2, q % 2
                if v == 0:
                    r0 = kt * TT + 1
                    return C.out[r0:r0 + TT, :].rearrange("(s p) d -> p s d", p=128), [C.outR[kt], C.outR[min(kt + 1, NT - 1)]]
                base = (NT - 1 - kt) * TT
                return C.out[base:base + TT, :].rearrange("(s p) d -> p s d", p=128), [C.outR[NT - 1 - kt]]

            def ldx(q):
                ap_, regs = rows(q)
                P.op("sp", lambda e: e.dma_start(out=xres[q % 3][:], in_=ap_), reads=regs, writes=[xresR[q % 3]], dma=True)

            def ldt(kt):
                b = kt % 2
                P.op("sp", lambda e: e.dma_start(out=Ct[b][:], in_=C.dftC[kt]), writes=[CtR[b]], dma=True)
                P.op("act", lambda e: e.dma_start(out=St[b][:], in_=C.dftS[kt]), writes=[StR[b]], dma=True)

            for cc in range(8):
                for n_ in range(32):
                    P.op("pe", lambda e, n_=n_, cc=cc: e.matmul(C.pb[0][:, 0:2], lhsT=hTM[:, n_, cc * 128:(cc + 1) * 128], rhs=T0[:, n_, :],
                                                              start=(n_ == 0), stop=(n_ == 31)),
                         reads=[hR[n_ // 2], cR], writes=[C.pbR[0]])
                P.op("act", lambda e, cc=cc: e.activation(out=Z0[:, cc, :], in_=C.pb[0][:, 0:2], func=AF.Identity), reads=[C.pbR[0]], writes=[Z0R])
            for cg in range(4):
                for cj in range(2):
                    for ci in range(2):
                        P.op("pe", lambda e, cg=cg, cj=cj, ci=ci: e.matmul(C.pb[1][:, 0:2], lhsT=ccC[:, ci, cj * 128:(cj + 1) * 128], rhs=Z0[:, cg * 2 + ci, :],
                                                                         start=(ci == 0), stop=(ci == 1)),
                             reads=[cR, Z0R], writes=[C.pbR[1]])
                    P.op("act", lambda e, cg=cg, cj=cj: e.activation(out=f0[:, cg * 2 + cj, :], in_=C.pb[1][:, 0:2], func=AF.Identity),
                         reads=[C.pbR[1]], writes=[f0R])
            for hf in range(2):
                for c in range(8):
                    P.op("pe", lambda e, c=c, hf=hf: e.matmul(C.pb[2 + hf][0:2, :], lhsT=f0[:, c, :], rhs=wfo[:, c, hf * 512:(hf + 1) * 512],
                                                            start=(c == 0), stop=(c == 7)),
                         reads=[f0R, wfoR], writes=[C.pbR[2 + hf]])
                P.op("dve", lambda e, hf=hf: e.tensor_tensor(out=ty[hf][0:1, :], in0=C.pb[2 + hf][0:1, :], in1=gate[0:1, hf * 512:(hf + 1) * 512], op=ALU.mult),
                     reads=[C.pbR[2 + hf], gateR], writes=[tyR[hf]])
                P.op("pool", lambda e, hf=hf: e.tensor_tensor(out=x0[0:1, hf * 512:(hf + 1) * 512], in0=x0[0:1, hf * 512:(hf + 1) * 512], in1=ty[hf][0:1, :], op=ALU.add),
                     reads=[tyR[hf], x0R], writes=[x0R])
            P.op("pool", lambda e: e.tensor_tensor(out=x0[0:1, :], in0=x0[0:1, :], in1=gb[0:1, :], op=ALU.add), reads=[x0R, gbR], writes=[x0R])
            P.op("sp", lambda e: e.dma_start(out=C.out[0:1, :], in_=x0[:]), reads=[x0R], dma=True)

            ldt(0)
            ldx(0)
            for kt in range(NKT):
                b = kt % 2
                if kt + 1 < NKT:
                    ldt(kt + 1)
                for cc in range(8):
                    pr = (cc % 2) * 2
                    pi = pr + 1
                    for n_ in range(32):
                        P.op("pe", lambda e, n_=n_, cc=cc, pr=pr, b=b: e.matmul(
                            C.pb[pr][:, 0:TT], lhsT=hTM[:, n_, cc * 128:(cc + 1) * 128], rhs=Ct[b][:, n_, :], start=(n_ == 0), stop=(n_ == 31)),
                            reads=[hR[n_ // 2], CtR[b]], writes=[C.pbR[pr]])
                        P.op("pe", lambda e, n_=n_, cc=cc, pi=pi, b=b: e.matmul(
                            C.pb[pi][:, 0:TT], lhsT=hTM[:, n_, cc * 128:(cc + 1) * 128], rhs=St[b][:, n_, :], start=(n_ == 0), stop=(n_ == 31)),
                            reads=[hR[n_ // 2], StR[b]], writes=[C.pbR[pi]])
                    P.op("act", lambda e, cc=cc, pr=pr: e.activation(out=ZT[:, cc, :], in_=C.pb[pr][:, 0:TT], func=AF.Identity),
                         reads=[C.pbR[pr]], writes=[ZTR[cc]])
                    P.op("dve", lambda e, cc=cc, pi=pi: e.tensor_copy(out=ZT[:, 8 + cc, :], in_=C.pb[pi][:, 0:TT]),
                         reads=[C.pbR[pi]], writes=[ZTR[8 + cc]])
                for v in range(2):
                    q = kt * 2 + v
                    xb = q % 3
                    if q + 1 < 2 * NKT:
                        ldx(q + 1)
                    wS = ccS if v == 0 else ccSn
                    for cg in range(4):
                        for cj in range(2):
                            pf = 4 + ((cg * 2 + cj) % 2)
                            ops4 = [(ccC, 0, cg * 2 + 0), (ccC, 1, cg * 2 + 1), (wS, 0, 8 + cg * 2 + 0), (wS, 1, 8 + cg * 2 + 1)]
                            for qq, (wm, ci, zi) in enumerate(ops4):
                                P.op("pe", lambda e, wm=wm, ci=ci, zi=zi, cj=cj, pf=pf, qq=qq: e.matmul(
                                    C.pb[pf][:, 0:TT], lhsT=wm[:, ci, cj * 128:(cj + 1) * 128], rhs=ZT[:, zi, :], start=(qq == 0), stop=(qq == 3)),
                                    reads=[cR, ZTR[zi]], writes=[C.pbR[pf]])
                            if v == 0:
                                P.op("act", lambda e, cg=cg, cj=cj, pf=pf: e.activation(out=fT[0][:, cg * 2 + cj, :], in_=C.pb[pf][:, 0:TT], func=AF.Identity),
                                     reads=[C.pbR[pf]], writes=[fTR[0]])
                            else:
                                P.op("act", lambda e, cg=cg, cj=cj, pf=pf: e.activation(out=fT[1][:, cg * 2 + cj, ::-1], in_=C.pb[pf][:, 0:TT], func=AF.Identity),
                                     reads=[C.pbR[pf]], writes=[fTR[1]])
                    for s_ in range(2):
                        for hf in range(2):
                            pbi = 6 + ((s_ * 2 + hf) % 2)
                            for c in range(8):
                                P.op("pe", lambda e, c=c, s_=s_, hf=hf, pbi=pbi, v=v: e.matmul(
                                    C.pb[pbi][:, :], lhsT=fT[v][:, c, s_ * 128:(s_ + 1) * 128], rhs=wfo[:, c, hf * 512:(hf + 1) * 512], start=(c == 0), stop=(c == 7)),
                                    reads=[fTR[v], wfoR], writes=[C.pbR[pbi]])
                            tb = (s_ * 2 + hf) % 2
                            P.op("dve", lambda e, pbi=pbi, tb=tb, hf=hf: e.tensor_tensor(out=ty[tb][:], in0=C.pb[pbi][:, :], in1=gate[:, hf * 512:(hf + 1) * 512], op=ALU.mult),
                                 reads=[C.pbR[pbi], gateR], writes=[tyR[tb]])
                            P.op("pool", lambda e, tb=tb, s_=s_, hf=hf, xb=xb: e.tensor_tensor(
                                out=xres[xb][:, s_, hf * 512:(hf + 1) * 512], in0=xres[xb][:, s_, hf * 512:(hf + 1) * 512], in1=ty[tb][:], op=ALU.add),
                                reads=[tyR[tb], xresR[xb]], writes=[xresR[xb]])
                        P.op("pool", lambda e, s_=s_, xb=xb: e.tensor_tensor(out=xres[xb][:, s_, :], in0=xres[xb][:, s_, :], in1=gb[:], op=ALU.add),
                             reads=[xresR[xb], gbR], writes=[xresR[xb]])
                    ap_, regs = rows(q)
                    if v == 0 and kt == NKT - 1:
                        r0 = kt * TT + 1
                        P.op("sp", lambda e, xb=xb, r0=r0: e.dma_start(out=C.out[r0:r0 + 128, :], in_=xres[xb][:, 0, :]),
                             reads=[xresR[xb]], writes=regs, dma=True)
                        P.op("sp", lambda e, xb=xb, r0=r0: e.dma_start(out=C.out[r0 + 128:r0 + 255, :], in_=xres[xb][0:127, 1, :]),
                             reads=[xresR[xb]], writes=regs, dma=True)
                    else:
                        P.op("sp", lambda e, xb=xb, ap_=ap_: e.dma_start(out=ap_, in_=xres[xb][:]), reads=[xresR[xb]], writes=regs, dma=True)
            P.barrier()
            P.emit()


def _consts():
    ident = np.eye(128, dtype=np.float32)
    F = 16
    inv = (10000.0 ** (-np.arange(F, dtype=np.float32) / F)).astype(np.float32)
    tok = np.arange(N)
    pos = [tok // 64, tok % 64]
    rc = np.zeros((128, N), np.float32)
    rs = np.zeros((128, N), np.float32)
    for m in range(2):
        for ax in range(2):
            ang = (pos[ax][None, :].astype(np.float32) * inv[:, None]).astype(np.float32)
            for hf in range(2):
                p0 = m * 64 + ax * 32 + hf * 16
                rc[p0:p0 + F] = np.cos(ang)
                rs[p0:p0 + F] = np.sin(ang) * (-1.0 if hf == 0 else 1.0)
    n = np.arange(N, dtype=np.int64)
    kk = np.arange(1, N // 2 + 1, dtype=np.int64)
    nk = (n[:, None] * kk[None, :]) % N
    ang = 2.0 * np.pi * nk.astype(np.float64) / N
    Cm = (np.cos(ang) / 64.0)
    Sm = (-np.sin(ang) / 64.0)

    def lay(M):
        return np.ascontiguousarray(M.reshape(32, 128, NT // 2, TT).transpose(2, 1, 0, 3)).astype(ml_dtypes.bfloat16)

    dC, dS = lay(Cm), lay(Sm)
    d0 = np.full((128, 32, 2), 1.0 / 64.0, dtype=np.float64).astype(ml_dtypes.bfloat16)
    c = np.arange(256, dtype=np.int64)
    a2 = 2.0 * np.pi * ((c[:, None] * c[None, :]) % 256).astype(np.float64) / 256
    cC = np.cos(a2) / 16.0
    cS = np.sin(a2) / 16.0

    def lay2(M):
        return np.ascontiguousarray(M.reshape(2, 128, 256).transpose(1, 0, 2)).astype(ml_dtypes.bfloat16)

    return dict(ident=ident, rope_c=rc, rope_s=rs, dftC=dC, dftS=dS, dft0=d0, ccC=lay2(cC), ccS=lay2(cS), ccSn=lay2(-cS))


def _rot_perm():
    idx = np.arange(1024).reshape(2, 4, 2, 2, 2, 16)
    return idx[:, :, :, :, ::-1, :].reshape(-1)


def make_in_maps(inputs, n_cores=8):
    f = lambda a: np.ascontiguousarray(np.asarray(a, dtype=np.float32))
    x, c, ctx, c_ctx = f(inputs["x"]), f(inputs["c"]), f(inputs["ctx"]), f(inputs["c_ctx"])
    consts = _consts()
    w_in = f(inputs["mix_w_in"])[0]
    perm = _rot_perm()
    shared = dict(
        ada_w=f(inputs["ada_w"]),
        ada_bc=np.ascontiguousarray(f(inputs["ada_b"]).reshape(2, 72, 128).transpose(0, 2, 1)),
        w_gu=f(inputs["ffn_w_gu"]), w_dn=f(inputs["ffn_w_down"]),
        w_in=w_in, w_rot=np.ascontiguousarray(w_in[:, :1024][:, perm]),
        w_out=f(inputs["mix_w_out"])[0],
        lam=f(inputs["diff_lambda"]).reshape(1, 256),
        subg=f(inputs["diff_subln_g"]).reshape(1, 128),
        sgu_g=f(inputs["sgu_norm_g"]).reshape(1, 512), sgu_bn=f(inputs["sgu_norm_b"]).reshape(1, 512),
        sgu_wT=np.ascontiguousarray(f(inputs["sgu_w"])[0].transpose(0, 2, 1)),
        sgu_b=f(inputs["sgu_b"]).reshape(1, 512),
        w_fo=f(inputs["fourier_w_out"])[0], b_fo=f(inputs["fourier_b_out"]).reshape(1, 1024),
        fng=f(inputs["final_norm_g"]).reshape(1, 1024),
        **consts,
    )
    maps = []
    for b in range(n_cores):
        cc = np.stack([c[b], c_ctx], axis=-1)
        cc = np.ascontiguousarray(cc.reshape(8, 128, 2).transpose(1, 0, 2))
        m = dict(shared)
        m.update(x=x[b], ctx=ctx[b], cc=cc)
        maps.append(m)
    return maps


def kernel(**inputs):
    nc = build()
    maps = make_in_maps(inputs, 8)
    res = run_bass_kernel_spmd(nc, maps, core_ids=list(range(8)))
    return np.stack([np.asarray(r["out"], dtype=np.float32) for r in res.results], axis=0)
```

```python
import math
import numpy as np
import ml_dtypes
from contextlib import ExitStack
import concourse.bass as bass
import concourse.mybir as mybir
from concourse.bass_utils import run_bass_kernel_spmd

F32 = mybir.dt.float32
BF16 = mybir.dt.bfloat16
AF = mybir.ActivationFunctionType
ALU = mybir.AluOpType

N, D, CTX, FF = 4096, 1024, 256, 2816
TT = 256
NT = N // TT
EPS = 1e-6
NKC = 34


class Reg:
    __slots__ = ("name", "ws", "rs")

    def __init__(self, name=""):
        self.name = name
        self.ws = []
        self.rs = []


class Op:
    __slots__ = ("eng", "fn", "deps", "sig", "ms", "dma", "dsem", "dval", "phase")


class Prog:
    ENGS = ("pe", "act", "dve", "pool", "sp")
    NDSEM = 40
    NDSEM_SW = 12

    def __init__(self, nc, es, same_engine_sync=True):
        self.nc = nc
        self.same = same_engine_sync
        self.esem = {e: es.enter_context(nc.semaphore("s_" + e)) for e in self.ENGS}
        self.dsems = [es.enter_context(nc.semaphore("d%d" % i)) for i in range(self.NDSEM + self.NDSEM_SW)]
        self.ms_count = {e: 0 for e in self.ENGS}
        self.seen = {e: {f: 0 for f in self.ENGS} for e in self.ENGS}
        self.seen_d = {e: [0] * (self.NDSEM + self.NDSEM_SW) for e in self.ENGS}
        self.ndma = 0
        self.dma_hist = []
        self.ndma_sw = 0
        self.dma_hist_sw = []
        self.ops = []
        self.n_inst = 0
        self.n_wait = 0
        self.phase = 0

    def op(self, eng, fn, reads=(), writes=(), dma=False, relax=False):
        o = Op()
        o.eng = eng
        o.fn = fn
        o.sig = False
        o.ms = None
        o.dma = dma
        o.phase = self.phase
        deps = []
        raw_ids = set()
        for r in reads:
            for wr in r.ws:
                deps.append(wr)
                raw_ids.add(id(wr))
        for w in writes:
            deps.extend(w.ws)
            deps.extend(w.rs)
        if dma and eng == "pool":
            j = self.ndma_sw
            self.ndma_sw += 1
            o.dsem = self.NDSEM + j % self.NDSEM_SW
            o.dval = 16 * (j // self.NDSEM_SW + 1)
            if j >= self.NDSEM_SW:
                deps.append(self.dma_hist_sw[j - self.NDSEM_SW])
            self.dma_hist_sw.append(o)
        elif dma:
            j = self.ndma
            self.ndma += 1
            o.dsem = j % self.NDSEM
            o.dval = 16 * (j // self.NDSEM + 1)
            if j >= self.NDSEM:
                deps.append(self.dma_hist[j - self.NDSEM])
            self.dma_hist.append(o)
        dd = []
        ids = set()
        for d in deps:
            if id(d) in ids or d is o or d.phase < self.phase:
                continue
            ids.add(id(d))
            if (not d.dma) and d.eng == eng and (eng == "pe" or not self.same or (relax and id(d) not in raw_ids)):
                continue
            dd.append(d)
            if not d.dma:
                d.sig = True
        o.deps = dd
        for r in reads:
            r.rs.append(o)
        for w in writes:
            if w.rs:
                w.ws = [o]
                w.rs = []
            else:
                if not dma:
                    w.ws = [x for x in w.ws if x.dma or x.eng != eng]
                w.ws.append(o)
        self.ops.append(o)
        return o

    def barrier(self):
        lasts = []
        for e in self.ENGS:
            for o in reversed(self.ops):
                if o.eng == e and not o.dma and o.fn is not None:
                    lasts.append(o)
                    break
        latest = {}
        for o in self.ops:
            if o.dma:
                latest[o.dsem] = o
        for e in self.ENGS:
            b = Op()
            b.eng = e
            b.fn = None
            b.sig = False
            b.ms = None
            b.dma = False
            b.phase = self.phase
            b.deps = [l for l in lasts if (l.eng != e or (self.same and e != "pe"))] + list(latest.values())
            for d in b.deps:
                if not d.dma:
                    d.sig = True
            self.ops.append(b)

    def emit(self):
        nc = self.nc
        ops = self.ops
        self.ops = []
        self.phase += 1
        for o in ops:
            if o.sig and not o.dma:
                self.ms_count[o.eng] += 1
                o.ms = self.ms_count[o.eng]
        by_eng = {e: [o for o in ops if o.eng == e] for e in self.ENGS}
        prog = self

        def run(eng_name, engine):
            seen = prog.seen[eng_name]
            seen_d = prog.seen_d[eng_name]
            for o in by_eng[eng_name]:
                for d in o.deps:
                    if d.dma:
                        if seen_d[d.dsem] < d.dval:
                            engine.wait_ge(prog.dsems[d.dsem], d.dval)
                            seen_d[d.dsem] = d.dval
                            prog.n_wait += 1
                    else:
                        if seen[d.eng] < d.ms:
                            engine.wait_ge(prog.esem[d.eng], d.ms)
                            seen[d.eng] = d.ms
                            prog.n_wait += 1
                if o.fn is None:
                    continue
                ins = o.fn(engine)
                prog.n_inst += 1
                if o.dma:
                    ins.then_inc(prog.dsems[o.dsem], 16)
                elif o.sig:
                    ins.then_inc(prog.esem[eng_name], 1)

        with nc.Block() as block:
            @block.tensor
            def _(e):
                run("pe", e)

            @block.scalar
            def _(e):
                run("act", e)

            @block.vector
            def _(e):
                run("dve", e)

            @block.gpsimd
            def _(e):
                run("pool", e)

            @block.sync
            def _(e):
                run("sp", e)


class Ctx:
    pass


def build(stop_after=99, same_sync=True):
    nc = bass.Bass("TRN2", target_bir_lowering=False)
    C = Ctx()
    C.nc = nc

    def din(name, shape, dt=F32):
        return nc.dram_tensor(name, shape, dt, kind="ExternalInput").ap()

    C.x_in = din("x", [N, D])
    C.ctx_in = din("ctx", [CTX, D])
    C.cc_in = din("cc", [128, 8, 2])
    C.ada_w = din("ada_w", [2, 1024, 9216])
    C.ada_bc = din("ada_bc", [2, 128, 72])
    C.w_gu = din("w_gu", [2, 2, 1024, 2 * FF])
    C.w_dn = din("w_dn", [2, 2, FF, 1024])
    C.w_in = din("w_in", [1024, 2560])
    C.w_rot = din("w_rot", [1024, 1024])
    C.w_out = din("w_out", [1024, 1024])
    C.lam_in = din("lam", [1, 256])
    C.subg = din("subg", [1, 128])
    C.sgu_g = din("sgu_g", [1, 512])
    C.sgu_bn = din("sgu_bn", [1, 512])
    C.sgu_wT = din("sgu_wT", [4, 128, 128])
    C.sgu_b = din("sgu_b", [1, 512])
    C.w_fo = din("w_fo", [1024, 1024])
    C.b_fo = din("b_fo", [1, 1024])
    C.fng = din("fng", [1, 1024])
    C.ident_in = din("ident", [128, 128])
    C.rope_c = din("rope_c", [128, N])
    C.rope_s = din("rope_s", [128, N])
    C.dftC = din("dftC", [NT // 2, 128, 32, TT], BF16)
    C.dftS = din("dftS", [NT // 2, 128, 32, TT], BF16)
    C.dft0 = din("dft0", [128, 32, 2], BF16)
    C.ccSn = din("ccSn", [128, 2, 256], BF16)
    C.ccC = din("ccC", [128, 2, 256], BF16)
    C.ccS = din("ccS", [128, 2, 256], BF16)
    C.out = nc.dram_tensor("out", [N, D], F32, kind="ExternalOutput").ap()
    C.cx = nc.dram_tensor("cx_s", [CTX, D], F32).ap()
    C.QTs = nc.dram_tensor("qt_s", [128, 4, N], BF16).ap()
    C.SGs = nc.dram_tensor("sg_s", [128, 4, N], BF16).ap()

    with ExitStack() as es:
        P = Prog(nc, es, same_engine_sync=same_sync)
        C.P = P
        C.pb = [es.enter_context(nc.psum_tensor("pb%d" % i, [128, 512], F32)) for i in range(8)]
        C.pbR = [Reg("pb%d" % i) for i in range(8)]

        cnt = [0]

        def T(name, shape, dt, stack=es):
            cnt[0] += 1
            return stack.enter_context(nc.sbuf_tensor("%s_t%d" % (name, cnt[0]), shape, dt))

        C.T = T
        C.ident = T("ident", [128, 128], F32)
        C.ones = T("ones", [128, 128], F32)
        C.mhalf = T("mhalf", [128, 8], F32)
        C.modcol = [T("modcol%d" % l, [128, 72, 2], F32) for l in range(2)]
        C.sc1p = [T("sc1p%d" % l, [128, 72, 2], F32) for l in range(2)]
        C.R_const = Reg("const")
        C.R_mod = Reg("mod")
        C.outR = [Reg("out%d" % t) for t in range(NT)]

        P.op("sp", lambda e: e.dma_start(out=C.ident[:], in_=C.ident_in[:, :]), writes=[C.R_const], dma=True)
        P.op("pool", lambda e: e.memset(C.ones[:], 1.0), writes=[C.R_const])
        P.op("pool", lambda e: e.memset(C.mhalf[:], -0.5), writes=[C.R_const])

        phase_ada(C)
        plan = [
            lambda: phase_ffn(C, 0, 0, C.x_in, with_ctx=True),
            lambda: phase_mix0(C),
            lambda: phase_ffn(C, 0, 1, C.out),
            lambda: phase_ffn(C, 1, 0, C.out),
            lambda: phase_fourier(C),
            lambda: phase_ffn(C, 1, 1, C.out, final=True),
        ]
        for i, ph in enumerate(plan):
            if i >= stop_after:
                break
            ph()
        print("bass build: inst=%d waits=%d dmas=%d" % (P.n_inst, P.n_wait, P.ndma))
    return nc


def phase_ada(C):
    P, nc = C.P, C.nc
    with ExitStack() as es:
        T = lambda n, s, d: C.T(n, s, d, es)
        cc = T("cc", [128, 8, 2], F32)
        sc = T("sc", [128, 8, 2], BF16)
        abc = [T("abc%d" % l, [128, 72], F32) for l in range(2)]
        NB = 4
        wt = [T("adaw%d" % i, [128, 8, 1024], F32) for i in range(NB)]
        wtR = [Reg("adaw%d" % i) for i in range(NB)]
        wb = [T("adab%d" % i, [128, 8, 1024], BF16) for i in range(2)]
        wbR = [Reg("adab%d" % i) for i in range(2)]
        R_cc, R_sc, R_abc = Reg(), Reg(), Reg()
        P.op("sp", lambda e: e.dma_start(out=cc[:], in_=C.cc_in[:, :, :]), writes=[R_cc], dma=True)
        for l in range(2):
            P.op("sp", lambda e, l=l: e.dma_start(out=abc[l][:], in_=C.ada_bc[l, :, :]), writes=[R_abc], dma=True)
        P.op("act", lambda e: e.activation(out=sc[:], in_=cc[:], func=AF.Silu), reads=[R_cc], writes=[R_sc])
        blk = 0
        for l in range(2):
            wv = C.ada_w[l].rearrange("(kc p) n -> p kc n", p=128)
            for j in range(9):
                b = blk % NB
                b2 = blk % 2
                blk += 1
                for hh in range(4):
                    P.op("sp" if hh % 2 == 0 else "act",
                         lambda e, b=b, j=j, wv=wv, hh=hh: e.dma_start(out=wt[b][:, hh * 2:(hh + 1) * 2, :],
                                                                      in_=wv[:, hh * 2:(hh + 1) * 2, j * 1024:(j + 1) * 1024]),
                         writes=[wtR[b]], dma=True)
                P.op("dve", lambda e, b=b, b2=b2: e.tensor_copy(out=wb[b2][:, 0:3, :], in_=wt[b][:, 0:3, :]), reads=[wtR[b]], writes=[wbR[b2]])
                P.op("act", lambda e, b=b, b2=b2: e.activation(out=wb[b2][:, 3:6, :], in_=wt[b][:, 3:6, :], func=AF.Identity), reads=[wtR[b]], writes=[wbR[b2]])
                P.op("pool", lambda e, b=b, b2=b2: e.tensor_copy(out=wb[b2][:, 6:8, :], in_=wt[b][:, 6:8, :]), reads=[wtR[b]], writes=[wbR[b2]])
                for cchunk in range(8):
                    ci = j * 8 + cchunk
                    pbi = ci % 4
                    for kc in range(8):
                        P.op("pe", lambda e, b2=b2, kc=kc, cchunk=cchunk, pbi=pbi: e.matmul(
                            C.pb[pbi][:, 0:2], lhsT=wb[b2][:, kc, cchunk * 128:(cchunk + 1) * 128], rhs=sc[:, kc, :],
                            start=(kc == 0), stop=(kc == 7)),
                            reads=[wbR[b2], R_sc], writes=[C.pbR[pbi]])
                    P.op("dve", lambda e, l=l, ci=ci, pbi=pbi: e.tensor_scalar(
                        out=C.modcol[l][:, ci, :], in0=C.pb[pbi][:, 0:2], scalar1=abc[l][:, ci:ci + 1], scalar2=None,
                        op0=ALU.add), reads=[C.pbR[pbi], R_abc], writes=[C.R_mod])
        for l in range(2):
            P.op("dve", lambda e, l=l: e.tensor_scalar(out=C.sc1p[l][:], in0=C.modcol[l][:], scalar1=1.0, scalar2=None,
                                                       op0=ALU.add), reads=[C.R_mod], writes=[C.R_mod])
        P.barrier()
        P.emit()


def build_bc(C, dst, dstR, col_ap_fn, mult, pbis=(0, 1), tmp=None, tmpR=None):
    P = C.P
    for half in range(2):
        pbi = pbis[half]
        for q in range(4):
            kc = half * 4 + q
            t = tmp[kc % 2]
            P.op("dve", lambda e, t=t, kc=kc: e.tensor_scalar(out=t[:], in0=C.ones[:], scalar1=col_ap_fn(kc), scalar2=float(mult),
                                                             op0=ALU.mult, op1=ALU.mult),
                 reads=[C.R_mod, C.R_const], writes=[tmpR[kc % 2]])
            P.op("pe", lambda e, t=t, q=q, pbi=pbi: e.matmul(C.pb[pbi][:, q * 128:(q + 1) * 128], lhsT=t[:], rhs=C.ident[:],
                                                             start=True, stop=True),
                 reads=[tmpR[kc % 2], C.R_const], writes=[C.pbR[pbi]])
        P.op("act", lambda e, half=half, pbi=pbi: e.activation(out=dst[:, half * 512:(half + 1) * 512], in_=C.pb[pbi][:, :], func=AF.Identity),
             reads=[C.pbR[pbi]], writes=[dstR])


class Front:
    def __init__(self, C, es, nbuf=3):
        T = lambda n, s, d: C.T(n, s, d, es)
        self.C = C
        self.nbuf = nbuf
        self.xin = [T("xin%d" % i, [128, 2, 1024], F32) for i in range(nbuf)]
        self.xinR = [Reg("xin%d" % i) for i in range(nbuf)]
        self.xn = T("xn", [128, 2, 1024], F32)
        self.xnR = Reg("xn")
        self.hT = [T("hT%d" % i, [128, 8, TT], BF16) for i in range(2)]
        self.hTR = [Reg("hT%d" % i) for i in range(2)]
        self.ss = T("ss", [128, 2], F32)
        self.rstd = T("rstd", [128, 2], F32)
        self.ssR = Reg("ss")
        self.rstdR = Reg("rstd")
        self.cnt = 0

    def load(self, src_rows, b, srcR=None):
        C, P = self.C, self.C.P
        P.op("sp", lambda e: e.dma_start(out=self.xin[b][:], in_=src_rows.rearrange("(s p) d -> p s d", p=128)),
             reads=[srcR] if srcR is not None else [], writes=[self.xinR[b]], dma=True)

    def norm(self, b):
        C, P = self.C, self.C.P
        xin, xn = self.xin[b], self.xn
        for s in range(2):
            P.op("act", lambda e, s=s: e.activation(out=xn[:, s, :], in_=xin[:, s, :], func=AF.Square, accum_out=self.ss[:, s:s + 1]),
                 reads=[self.xinR[b]], writes=[self.xnR, self.ssR])
        P.op("pool", lambda e: e.tensor_scalar(out=self.rstd[:], in0=self.ss[:], scalar1=1.0 / D, scalar2=EPS, op0=ALU.mult, op1=ALU.add),
             reads=[self.ssR], writes=[self.rstdR])
        P.op("pool", lambda e: e.tensor_tensor(out=self.rstd[:], in0=self.rstd[:], in1=C.mhalf[:, 0:2], op=ALU.pow),
             reads=[self.rstdR, C.R_const], writes=[self.rstdR])
        for s in range(2):
            P.op("dve", lambda e, s=s: e.tensor_scalar(out=xn[:, s, :], in0=xin[:, s, :], scalar1=self.rstd[:, s:s + 1], scalar2=None, op0=ALU.mult),
                 reads=[self.xinR[b], self.rstdR], writes=[self.xnR])

    def transpose_mod(self, hb, l, ivec, col, pbis=(0, 1)):
        C, P = self.C, self.C.P
        xn = self.xn
        for kp in range(4):
            pbi = pbis[kp % 2]
            for q in range(2):
                kc = kp * 2 + q
                for s in range(2):
                    P.op("pe", lambda e, kc=kc, s=s, q=q, pbi=pbi: e.transpose(
                        out=C.pb[pbi][:, q * 256 + s * 128:q * 256 + (s + 1) * 128], in_=xn[:, s, kc * 128:(kc + 1) * 128], identity=C.ident[:]),
                        reads=[self.xnR, C.R_const], writes=[C.pbR[pbi]])
            for q in range(2):
                kc = kp * 2 + q
                P.op("act", lambda e, kc=kc, q=q, pbi=pbi: e.activation(
                    out=self.hT[hb][:, kc, :], in_=C.pb[pbi][:, q * 256:(q + 1) * 256], func=AF.Identity,
                    bias=C.modcol[l][:, ivec * 8 + kc, col:col + 1], scale=C.sc1p[l][:, (ivec + 1) * 8 + kc, col:col + 1]),
                    reads=[C.pbR[pbi], C.R_mod], writes=[self.hTR[hb]])


def phase_ffn(C, l, half, src, with_ctx=False, final=False):
    P, nc = C.P, C.nc
    ivec = 0 if half == 0 else 6
    with ExitStack() as es:
        T = lambda n, s, d: C.T(n, s, d, es)
        wgu = T("wgu", [128, 8, 2 * FF], BF16)
        wdn = T("wdn", [128, 22, 1024], BF16)
        wguR = [Reg("wgu%d" % k) for k in range(8)]
        wdnR = [Reg("wdn%d" % k) for k in range(2)]
        fr = Front(C, es, nbuf=3)
        aT = T("aT", [128, 22, TT], BF16)
        aTR = Reg("aT")
        sil = [T("sil%d" % i, [128, TT], F32) for i in range(3)]
        silR = [Reg("sil%d" % i) for i in range(3)]
        ty = [T("ty%d" % i, [128, 512], F32) for i in range(2)]
        tyR = [Reg("ty%d" % i) for i in range(2)]
        gate = T("gate", [128, 1024], F32)
        gateR = Reg("gate")
        gtmp = [T("gtmp%d" % i, [128, 128], F32) for i in range(2)]
        gtmpR = [Reg(), Reg()]
        if with_ctx:
            gatec = T("gatec", [128, 1024], F32)
            gatecR = Reg("gatec")
        if final:
            gfin = T("gfin", [128, 1024], F32)
            gfinR = Reg("gfin")
            ss2 = T("ss2", [128, 2], F32)
            ss2R = Reg("ss2")
            rs2 = T("rs2", [128, 2], F32)
            rs2R = Reg("rs2")
            fjunk = T("fjunk", [128, 1024], BF16)
            fjunkR = Reg("fjunk")

        wg_src = C.w_gu[l, half]
        for kc in range(8):
            P.op("pool", lambda e, kc=kc: e.dma_start(out=wgu[:, kc, :], in_=wg_src[kc * 128:(kc + 1) * 128, :]),
                 writes=[wguR[kc]], dma=True)
        wd_src = C.w_dn[l, half].rearrange("(j p) d -> p j d", p=128)
        for k in range(2):
            P.op("pool", lambda e, k=k: e.dma_start(out=wdn[:, k * 11:(k + 1) * 11, :], in_=wd_src[:, k * 11:(k + 1) * 11, :]),
                 writes=[wdnR[k]], dma=True)

        tiles = []
        if with_ctx:
            tiles.append((C.ctx_in[0:CTX, :], C.cx[0:CTX, :], 1, None, None))
        for t in range(NT):
            tiles.append((src[t * TT:(t + 1) * TT, :], C.out[t * TT:(t + 1) * TT, :], 0, C.outR[t] if src is C.out else None, C.outR[t]))
        ntl = len(tiles)

        build_bc(C, gate, gateR, lambda kc: C.modcol[l][:, (ivec + 2) * 8 + kc, 0:1], 0.5, pbis=(6, 7), tmp=gtmp, tmpR=gtmpR)
        if with_ctx:
            build_bc(C, gatec, gatecR, lambda kc: C.modcol[l][:, (ivec + 2) * 8 + kc, 1:2], 0.5, pbis=(6, 7), tmp=gtmp, tmpR=gtmpR)
        if final:
            P.op("sp", lambda e: e.dma_start(out=gfin[:], in_=C.fng[0:1, :].partition_broadcast(128)), writes=[gfinR], dma=True)

        for i in range(min(3, ntl)):
            fr.load(tiles[i][0], i % 3, tiles[i][3])
        fr.norm(0)
        fr.transpose_mod(0, l, ivec, tiles[0][2])
        def finish(i):
            xb = i % 3
            if final:
                for s in range(2):
                    P.op("act", lambda e, s=s: e.activation(out=fjunk[:], in_=fr.xin[xb][:, s, :], func=AF.Square, accum_out=ss2[:, s:s + 1]),
                         reads=[fr.xinR[xb]], writes=[fjunkR, ss2R], relax=True)
                P.op("pool", lambda e: e.tensor_scalar(out=rs2[:], in0=ss2[:], scalar1=1.0 / D, scalar2=EPS, op0=ALU.mult, op1=ALU.add),
                     reads=[ss2R], writes=[rs2R])
                P.op("pool", lambda e: e.tensor_tensor(out=rs2[:], in0=rs2[:], in1=C.mhalf[:, 0:2], op=ALU.pow),
                     reads=[rs2R, C.R_const], writes=[rs2R])
                for s in range(2):
                    P.op("dve", lambda e, s=s: e.scalar_tensor_tensor(
                        out=fr.xin[xb][:, s, :], in0=fr.xin[xb][:, s, :], scalar=rs2[:, s:s + 1], in1=gfin[:], op0=ALU.mult, op1=ALU.mult),
                        reads=[fr.xinR[xb], rs2R, gfinR], writes=[fr.xinR[xb]])
            dst = tiles[i][1]
            P.op("sp", lambda e: e.dma_start(out=dst.rearrange("(s p) d -> p s d", p=128), in_=fr.xin[xb][:]),
                 reads=[fr.xinR[xb]], writes=[tiles[i][4]] if tiles[i][4] is not None else [], dma=True)
            if i + 3 < ntl:
                fr.load(tiles[i + 3][0], xb, tiles[i + 3][3])

        for i in range(ntl):
            xb = i % 3
            hb = i % 2
            col = tiles[i][2]
            if i + 1 < ntl:
                fr.norm((i + 1) % 3)
            for j in range(22):
                pbi = 2 + (j % 4)
                for part in range(2):
                    for kc in range(8):
                        P.op("pe", lambda e, kc=kc, j=j, part=part, pbi=pbi, hb=hb: e.matmul(
                            C.pb[pbi][:, part * 256:(part + 1) * 256],
                            lhsT=wgu[:, kc, part * FF + j * 128: part * FF + (j + 1) * 128], rhs=fr.hT[hb][:, kc, :],
                            start=(kc == 0), stop=(kc == 7)),
                            reads=[wguR[kc], fr.hTR[hb]], writes=[C.pbR[pbi]])
                sb = j % 3
                P.op("act", lambda e, pbi=pbi, sb=sb: e.activation(out=sil[sb][:], in_=C.pb[pbi][:, 0:256], func=AF.Silu),
                     reads=[C.pbR[pbi]], writes=[silR[sb]])
                P.op("dve", lambda e, pbi=pbi, sb=sb, j=j: e.tensor_tensor(out=aT[:, j, :], in0=sil[sb][:], in1=C.pb[pbi][:, 256:512], op=ALU.mult),
                     reads=[silR[sb], C.pbR[pbi]], writes=[aTR])
            if final and i > 0:
                finish(i - 1)
            if i + 1 < ntl:
                fr.transpose_mod((i + 1) % 2, l, ivec, tiles[i + 1][2])
            gt = gatec if (with_ctx and col == 1) else gate
            gtR = gatecR if (with_ctx and col == 1) else gateR
            for s in range(2):
                for hf in range(2):
                    pbi = 6 + ((s * 2 + hf) % 2)
                    for j in range(22):
                        P.op("pe", lambda e, j=j, s=s, hf=hf, pbi=pbi: e.matmul(
                            C.pb[pbi][:, :], lhsT=aT[:, j, s * 128:(s + 1) * 128], rhs=wdn[:, j, hf * 512:(hf + 1) * 512],
                            start=(j == 0), stop=(j == 21)),
                            reads=[aTR, wdnR[j // 11]], writes=[C.pbR[pbi]])
                    tb = (s * 2 + hf) % 2
                    P.op("dve", lambda e, pbi=pbi, tb=tb, hf=hf, gt=gt: e.tensor_tensor(
                        out=ty[tb][:], in0=C.pb[pbi][:, :], in1=gt[:, hf * 512:(hf + 1) * 512], op=ALU.mult),
                        reads=[C.pbR[pbi], gtR], writes=[tyR[tb]])
                    P.op("pool", lambda e, tb=tb, s=s, hf=hf, xb=xb: e.tensor_tensor(
                        out=fr.xin[xb][:, s, hf * 512:(hf + 1) * 512], in0=fr.xin[xb][:, s, hf * 512:(hf + 1) * 512], in1=ty[tb][:], op=ALU.add),
                        reads=[tyR[tb], fr.xinR[xb]], writes=[fr.xinR[xb]])
            if not final:
                finish(i)
        if final:
            finish(ntl - 1)
        P.barrier()
        P.emit()


def phase_mix0(C):
    P, nc = C.P, C.nc
    l = 0
    LAM_INIT = 0.8 - 0.6 * math.exp(-0.3 * 0)
    with ExitStack() as es_kv:
        Tk = lambda n, s, d: C.T(n, s, d, es_kv)
        KT = Tk("KT", [128, 4, N + CTX], BF16)
        V = Tk("V", [128, NKC, 4, 130], BF16)
        KTR = [Reg("KT%d" % i) for i in range(NT + 1)]
        VR = [Reg("V%d" % i) for i in range(NT + 1)]
        VoneR = Reg("Vone")
        with ExitStack() as es:
            T = lambda n, s, d: C.T(n, s, d, es)
            w_in = T("w_in", [128, 8, 2560], BF16)
            w_rot = T("w_rot", [128, 8, 1024], BF16)
            wR = [Reg("w_in%d" % k) for k in range(8)]
            wrR = Reg("w_rot")
            fr = Front(C, es, nbuf=2)
            cs = [T("cs%d" % i, [128, 2, TT], F32) for i in range(2)]
            csR = [Reg(), Reg()]
            t1 = [T("t1_%d" % i, [128, TT], F32) for i in range(2)]
            t2 = [T("t2_%d" % i, [128, TT], F32) for i in range(2)]
            t1R = [Reg(), Reg()]
            t2R = [Reg(), Reg()]
            qT = [T("qT%d" % i, [128, 4, TT], BF16) for i in range(2)]
            qTR = [Reg(), Reg()]
            uT = T("uT", [128, 4, TT], F32)
            uTR = Reg("uT")
            vg = [T("vg%d" % i, [128, 512], F32) for i in range(2)]
            vgR = [Reg(), Reg()]
            vgc = [T("vgc%d" % i, [128, 512], F32) for i in range(2)]
            vgcR = [Reg(), Reg()]
            vgn = [T("vgn%d" % i, [128, 512], BF16) for i in range(2)]
            vgnR = [Reg(), Reg()]
            st2 = [T("st%d" % i, [128, 4, 2], F32) for i in range(2)]
            mv2 = [T("mv%d" % i, [128, 4, 2], F32) for i in range(2)]
            rs42 = [T("rs4%d" % i, [128, 4], F32) for i in range(2)]
            st2R, mv2R, rs42R = [Reg(), Reg()], [Reg(), Reg()], [Reg(), Reg()]
            sgT = [T("sgT%d" % i, [128, 4, TT], BF16) for i in range(2)]
            sgTR = [Reg(), Reg()]
            lng = T("lng", [128, 512], F32)
            lnb = T("lnb", [128, 512], F32)
            wsT = T("wsT", [128, 4, 128], BF16)
            bsr = T("bsr", [1, 512], BF16)
            onr = T("onr", [1, 128], BF16)
            cR = Reg("m1const")

            w_src = C.w_in.rearrange("(kc p) n -> p kc n", p=128)
            for kc in range(8):
                for hh in range(2):
                    P.op("pool", lambda e, kc=kc, hh=hh: e.dma_start(out=w_in[:, kc, hh * 1280:(hh + 1) * 1280],
                                                                  in_=C.w_in[kc * 128:(kc + 1) * 128, hh * 1280:(hh + 1) * 1280]),
                         writes=[wR[kc]], dma=True)
            wr_src = C.w_rot.rearrange("(kc p) n -> p kc n", p=128)
            for hh in range(2):
                P.op("pool", lambda e, hh=hh: e.dma_start(out=w_rot[:, hh * 4:(hh + 1) * 4, :], in_=wr_src[:, hh * 4:(hh + 1) * 4, :]),
                     writes=[wrR], dma=True)
            P.op("pool", lambda e: e.dma_start(out=wsT[:], in_=C.sgu_wT.rearrange("g q p -> q g p")), writes=[cR], dma=True)
            P.op("pool", lambda e: e.dma_start(out=bsr[:], in_=C.sgu_b[0:1, :]), writes=[cR], dma=True)
            P.op("sp", lambda e: e.dma_start(out=lng[:], in_=C.sgu_g[0:1, :].partition_broadcast(128)), writes=[cR], dma=True)
            P.op("sp", lambda e: e.dma_start(out=lnb[:], in_=C.sgu_bn[0:1, :].partition_broadcast(128)), writes=[cR], dma=True)
            P.op("dve", lambda e: e.memset(onr[:], 1.0), writes=[cR])
            P.op("pool", lambda e: e.memset(V[:, :, :, 128:130], 1.0), writes=[VoneR])

            tiles = [(C.cx[0:CTX, :], 1, NT)] + [(C.out[t * TT:(t + 1) * TT, :], 0, t) for t in range(NT)]
            ntl = len(tiles)
            rot = [0]

            def nextpb():
                rot[0] += 1
                return 2 + (rot[0] % 4)

            fr.load(tiles[0][0], 0)
            fr.load(tiles[1][0], 1)
            fr.norm(0)
            fr.transpose_mod(0, l, 3, tiles[0][1])
            for i in range(ntl):
                src, col, tix = tiles[i]
                hb = i % 2
                hT = fr.hT[hb]
                hTR = fr.hTR[hb]
                is_ctx = (col == 1)
                tok0 = N if is_ctx else tix * TT
                if i + 1 < ntl:
                    fr.norm((i + 1) % 2)
                if i + 2 < ntl:
                    fr.load(tiles[i + 2][0], i % 2)
                cb = i % 2
                if not is_ctx:
                    P.op("sp", lambda e, cb=cb, tok0=tok0: e.dma_start(out=cs[cb][:, 0, :], in_=C.rope_c[:, tok0:tok0 + TT]), writes=[csR[cb]], dma=True)
                    P.op("sp", lambda e, cb=cb, tok0=tok0: e.dma_start(out=cs[cb][:, 1, :], in_=C.rope_s[:, tok0:tok0 + TT]), writes=[csR[cb]], dma=True)
                def tile_body(hT, hTR, is_ctx, tok0, tix, cb, mid_hook):
                    def do_qk(t_, h):
                        pbi = nextpb()
                        c0 = t_ * 512 + h * 128
                        for kc in range(8):
                            P.op("pe", lambda e, kc=kc, c0=c0, pbi=pbi: e.matmul(
                                C.pb[pbi][:, 0:TT], lhsT=w_in[:, kc, c0:c0 + 128], rhs=hT[:, kc, :], start=(kc == 0), stop=(kc == 7)),
                                reads=[wR[kc], hTR], writes=[C.pbR[pbi]])
                        if is_ctx:
                            P.op("act", lambda e, pbi=pbi, h=h: e.activation(out=KT[:, h, tok0:tok0 + TT], in_=C.pb[pbi][:, 0:TT], func=AF.Identity),
                                 reads=[C.pbR[pbi]], writes=[KTR[tix]])
                            return
                        for kc in range(8):
                            P.op("pe", lambda e, kc=kc, c0=c0, pbi=pbi: e.matmul(
                                C.pb[pbi][:, TT:2 * TT], lhsT=w_rot[:, kc, c0:c0 + 128], rhs=hT[:, kc, :], start=(kc == 0), stop=(kc == 7)),
                                reads=[wrR, hTR], writes=[C.pbR[pbi]])
                        tb = h % 2
                        P.op("dve", lambda e, pbi=pbi, tb=tb: e.tensor_tensor(out=t1[tb][:], in0=C.pb[pbi][:, 0:TT], in1=cs[cb][:, 0, :], op=ALU.mult),
                             reads=[C.pbR[pbi], csR[cb]], writes=[t1R[tb]])
                        P.op("dve", lambda e, pbi=pbi, tb=tb: e.tensor_tensor(out=t2[tb][:], in0=C.pb[pbi][:, TT:2 * TT], in1=cs[cb][:, 1, :], op=ALU.mult),
                             reads=[C.pbR[pbi], csR[cb]], writes=[t2R[tb]])
                        if t_ == 1:
                            P.op("pool", lambda e, tb=tb, h=h: e.tensor_tensor(out=KT[:, h, tok0:tok0 + TT], in0=t1[tb][:], in1=t2[tb][:], op=ALU.add),
                                 reads=[t1R[tb], t2R[tb]], writes=[KTR[tix]])
                        else:
                            P.op("pool", lambda e, tb=tb, h=h: e.tensor_tensor(out=qT[cb][:, h, :], in0=t1[tb][:], in1=t2[tb][:], op=ALU.add),
                                 reads=[t1R[tb], t2R[tb]], writes=[qTR[cb]])

                    def do_v(s):
                        pbi = nextpb()
                        chunk = tok0 // 128 + s
                        for kc in range(8):
                            P.op("pe", lambda e, kc=kc, s=s, pbi=pbi: e.matmul(
                                C.pb[pbi][:, :], lhsT=hT[:, kc, s * 128:(s + 1) * 128], rhs=w_in[:, kc, 1024:1536], start=(kc == 0), stop=(kc == 7)),
                                reads=[wR[kc], hTR], writes=[C.pbR[pbi]])
                        P.op("act", lambda e, pbi=pbi, chunk=chunk: e.activation(
                            out=V[:, chunk, :, 0:128], in_=C.pb[pbi][:, :].rearrange("p (h e) -> p h e", h=4), func=AF.Identity),
                            reads=[C.pbR[pbi]], writes=[VR[tix]])

                    def do_u(g):
                        pbi = nextpb()
                        c0 = 1536 + g * 128
                        for kc in range(8):
                            P.op("pe", lambda e, kc=kc, c0=c0, pbi=pbi: e.matmul(
                                C.pb[pbi][:, 0:TT], lhsT=w_in[:, kc, c0:c0 + 128], rhs=hT[:, kc, :], start=(kc == 0), stop=(kc == 7)),
                                reads=[wR[kc], hTR], writes=[C.pbR[pbi]])
                        P.op("act", lambda e, pbi=pbi, g=g: e.activation(out=uT[:, g, :], in_=C.pb[pbi][:, 0:TT], func=AF.Gelu_apprx_tanh),
                             reads=[C.pbR[pbi]], writes=[uTR])

                    def do_vg(s):
                        pbi = nextpb()
                        for kc in range(8):
                            P.op("pe", lambda e, kc=kc, s=s, pbi=pbi: e.matmul(
                                C.pb[pbi][:, :], lhsT=hT[:, kc, s * 128:(s + 1) * 128], rhs=w_in[:, kc, 2048:2560], start=(kc == 0), stop=(kc == 7)),
                                reads=[wR[kc], hTR], writes=[C.pbR[pbi]])
                        P.op("act", lambda e, pbi=pbi, s=s: e.activation(out=vg[s][:], in_=C.pb[pbi][:, :], func=AF.Gelu_apprx_tanh),
                             reads=[C.pbR[pbi]], writes=[vgR[s]])
                        for g in range(4):
                            P.op("act", lambda e, g=g, s=s: e.activation(out=vgc[s][:, g * 128:(g + 1) * 128], in_=vg[s][:, g * 128:(g + 1) * 128],
                                                                       func=AF.Identity, accum_out=st2[s][:, g, 0:1]),
                                 reads=[vgR[s]], writes=[st2R[s], vgcR[s]])
                            P.op("act", lambda e, g=g, s=s: e.activation(out=vgc[s][:, g * 128:(g + 1) * 128], in_=vg[s][:, g * 128:(g + 1) * 128],
                                                                       func=AF.Square, accum_out=st2[s][:, g, 1:2]),
                                 reads=[vgR[s]], writes=[st2R[s], vgcR[s]])

                    def do_ln_small(s):
                        st, mv, rs4 = st2[s], mv2[s], rs42[s]
                        P.op("pool", lambda e: e.tensor_scalar(out=mv[:, :, 0], in0=st[:, :, 0], scalar1=1.0 / 128, scalar2=None, op0=ALU.mult),
                             reads=[st2R[s]], writes=[mv2R[s]])
                        P.op("pool", lambda e: e.tensor_tensor(out=mv[:, :, 1], in0=mv[:, :, 0], in1=mv[:, :, 0], op=ALU.mult),
                             reads=[mv2R[s]], writes=[mv2R[s]])
                        P.op("pool", lambda e: e.tensor_scalar(out=rs4[:], in0=st[:, :, 1], scalar1=1.0 / 128, scalar2=EPS, op0=ALU.mult, op1=ALU.add),
                             reads=[st2R[s]], writes=[rs42R[s]])
                        P.op("pool", lambda e: e.tensor_tensor(out=rs4[:], in0=rs4[:], in1=mv[:, :, 1], op=ALU.subtract),
                             reads=[rs42R[s], mv2R[s]], writes=[rs42R[s]])
                        P.op("pool", lambda e: e.tensor_tensor(out=rs4[:], in0=rs4[:], in1=C.mhalf[:, 0:4], op=ALU.pow),
                             reads=[rs42R[s], C.R_const], writes=[rs42R[s]])

                    def do_ln_big(s):
                        mv, rs4 = mv2[s], rs42[s]
                        for g in range(4):
                            P.op("dve", lambda e, g=g, s=s: e.tensor_scalar(
                                out=vgc[s][:, g * 128:(g + 1) * 128], in0=vg[s][:, g * 128:(g + 1) * 128],
                                scalar1=mv[:, g, 0:1], scalar2=rs4[:, g:g + 1], op0=ALU.subtract, op1=ALU.mult),
                                reads=[vgR[s], mv2R[s], rs42R[s]], writes=[vgcR[s]])
                        P.op("pool", lambda e, s=s: e.tensor_tensor(out=vgc[s][:], in0=vgc[s][:], in1=lng[:], op=ALU.mult),
                             reads=[vgcR[s], cR], writes=[vgcR[s]])
                        P.op("pool", lambda e, s=s: e.tensor_tensor(out=vgn[s][:], in0=vgc[s][:], in1=lnb[:], op=ALU.add),
                             reads=[vgcR[s], cR], writes=[vgnR[s]])

                    def do_mix(s):
                        pbm = 6 + (s % 2)
                        for g in range(4):
                            P.op("pe", lambda e, g=g, s=s, pbm=pbm: e.matmul(
                                C.pb[pbm][:, g * 128:(g + 1) * 128], lhsT=vgn[s][:, g * 128:(g + 1) * 128], rhs=wsT[:, g, :], start=True, stop=False),
                                reads=[vgnR[s], cR], writes=[C.pbR[pbm]])
                            P.op("pe", lambda e, g=g, pbm=pbm: e.matmul(
                                C.pb[pbm][:, g * 128:(g + 1) * 128], lhsT=onr[0:1, :], rhs=bsr[0:1, g * 128:(g + 1) * 128], start=False, stop=True),
                                reads=[cR], writes=[C.pbR[pbm]])
                        P.op("dve", lambda e, s=s, pbm=pbm: e.tensor_tensor(
                            out=sgT[cb][:, :, s * 128:(s + 1) * 128], in0=C.pb[pbm][:, :].rearrange("p (g q) -> p g q", g=4),
                            in1=uT[:, :, s * 128:(s + 1) * 128], op=ALU.mult),
                            reads=[C.pbR[pbm], uTR], writes=[sgTR[cb]])

                    if is_ctx:
                        for h in range(4):
                            do_qk(1, h)
                        mid_hook()
                        for s in range(2):
                            do_v(s)
                    else:
                        do_vg(0)
                        do_vg(1)
                        for h in range(4):
                            do_qk(0, h)
                        P.op("sp", lambda e, cb=cb, tok0=tok0: e.dma_start(out=C.QTs[:, :, tok0:tok0 + TT], in_=qT[cb][:]), reads=[qTR[cb]], dma=True)
                        do_ln_small(0)
                        do_ln_small(1)
                        do_ln_big(0)
                        mid_hook()
                        for h in range(4):
                            do_qk(1, h)
                        do_ln_big(1)
                        for s in range(2):
                            do_v(s)
                        for g in range(4):
                            do_u(g)
                        do_mix(0)
                        do_mix(1)
                        P.op("sp", lambda e, cb=cb, tok0=tok0: e.dma_start(out=C.SGs[:, :, tok0:tok0 + TT], in_=sgT[cb][:]), reads=[sgTR[cb]], dma=True)

                def mid_hook(i=i):
                    if i + 1 < ntl:
                        fr.transpose_mod((i + 1) % 2, l, 3, tiles[i + 1][1])

                tile_body(hT, hTR, is_ctx, tok0, tix, cb, mid_hook)
            P.barrier()
            P.emit()

        import os
        if os.environ.get('SKIP_M2'):
            return
        with ExitStack() as es:
            T = lambda n, s, d: C.T(n, s, d, es)
            w_out = T("w_out", [128, 8, 1024], BF16)
            woR = Reg("w_out")
            gate = T("gate5", [128, 1024], F32)
            gateR = Reg()
            gtmp = [T("gtmp%d" % i, [128, 128], F32) for i in range(2)]
            gtmpR = [Reg(), Reg()]
            gsub = T("gsub", [128, 128], F32)
            lamt = T("lamt", [128, 256], F32)
            lj = T("lj", [128, 64], F32)
            lsc = T("lsc", [128, 8], F32)
            cR = Reg("m2const")
            qz = [[T("qz%d_%d" % (m, i), [128, 4, TT], BF16) for i in range(2)] for m in range(2)]
            qTR = [Reg(), Reg()]
            for m in range(2):
                for i in range(2):
                    P.op("pool", lambda e, m=m, i=i: e.memset(qz[m][i][:], 0.0), writes=[qTR[i]])
            catT = [T("catT%d" % i, [128, 8, TT], BF16) for i in range(2)]
            catAR = [Reg(), Reg()]
            catSR = [Reg(), Reg()]
            xres = [T("xres%d" % i, [128, 2, 1024], F32) for i in range(2)]
            xresR = [Reg(), Reg()]
            pT = [T("pT%d" % i, [128, 2 * TT], BF16) for i in range(2)]
            pTR = [Reg(), Reg()]
            ty = [T("ty%d" % i, [128, 512], F32) for i in range(2)]
            tyR = [Reg(), Reg()]

            wo_src = C.w_out.rearrange("(c p) d -> p c d", p=128)
            for hh in range(2):
                P.op("pool", lambda e, hh=hh: e.dma_start(out=w_out[:, hh * 4:(hh + 1) * 4, :], in_=wo_src[:, hh * 4:(hh + 1) * 4, :]),
                     writes=[woR], dma=True)
            P.op("sp", lambda e: e.dma_start(out=gsub[:], in_=C.subg[0:1, :].partition_broadcast(128)), writes=[cR], dma=True)
            P.op("sp", lambda e: e.dma_start(out=lamt[:], in_=C.lam_in[0:1, :].partition_broadcast(128)), writes=[cR], dma=True)
            P.op("dve", lambda e: e.tensor_scalar(out=gsub[:], in0=gsub[:], scalar1=float(1.0 - LAM_INIT), scalar2=None, op0=ALU.mult),
                 reads=[cR], writes=[cR])
            P.op("dve", lambda e: e.tensor_tensor(out=lj[:], in0=lamt[:, 0:64], in1=lamt[:, 64:128], op=ALU.mult), reads=[cR], writes=[cR])
            P.op("act", lambda e: e.activation(out=lj[:], in_=lj[:], func=AF.Identity, accum_out=lsc[:, 0:1]), reads=[cR], writes=[cR])
            P.op("dve", lambda e: e.tensor_tensor(out=lj[:], in0=lamt[:, 128:192], in1=lamt[:, 192:256], op=ALU.mult), reads=[cR], writes=[cR])
            P.op("act", lambda e: e.activation(out=lj[:], in_=lj[:], func=AF.Identity, accum_out=lsc[:, 1:2]), reads=[cR], writes=[cR])
            P.op("act", lambda e: e.activation(out=lsc[:, 2:4], in_=lsc[:, 0:2], func=AF.Exp), reads=[cR], writes=[cR])
            P.op("dve", lambda e: e.tensor_tensor(out=lsc[:, 4:5], in0=lsc[:, 2:3], in1=lsc[:, 3:4], op=ALU.subtract), reads=[cR], writes=[cR])
            P.op("dve", lambda e: e.tensor_scalar(out=lsc[:, 5:6], in0=lsc[:, 4:5], scalar1=-1.0, scalar2=float(-LAM_INIT), op0=ALU.mult, op1=ALU.add),
                 reads=[cR], writes=[cR])
            build_bc(C, gate, gateR, lambda kc: C.modcol[l][:, 5 * 8 + kc, 0:1], 1.0, pbis=(2, 3), tmp=gtmp, tmpR=gtmpR)

            def loads(t):
                b = t % 2
                tok0 = t * TT
                for m in range(2):
                    P.op("sp", lambda e, m=m: e.dma_start(out=qz[m][b][m * 64:(m + 1) * 64, :, :], in_=C.QTs[m * 64:(m + 1) * 64, :, tok0:tok0 + TT]),
                         writes=[qTR[b]], dma=True)
                P.op("sp", lambda e: e.dma_start(out=catT[b][:, 4:8, :], in_=C.SGs[:, :, tok0:tok0 + TT]), writes=[catSR[b]], dma=True)
                P.op("sp", lambda e: e.dma_start(out=xres[b][:], in_=C.out[tok0:tok0 + TT, :].rearrange("(s p) d -> p s d", p=128)),
                     reads=[C.outR[t]], writes=[xresR[b]], dma=True)

            steps = [(h, kc) for h in range(4) for kc in range(NKC)]
            nst = len(steps)
            rr = [T("rr%d" % i, [128, 4], F32) for i in range(2)]
            rrR = [[Reg() for _ in range(3)] for _ in range(2)]
            ot1 = [T("ot1_%d" % i, [128, 128], F32) for i in range(2)]
            ot1R = [Reg(), Reg()]
            oo = [T("oo%d" % i, [128, 128], F32) for i in range(2)]
            ooR = [Reg(), Reg()]
            oj = [T("oj%d" % i, [128, 128], F32) for i in range(2)]
            ojR = [Reg(), Reg()]
            oss = [T("oss%d" % i, [128, 2], F32) for i in range(2)]
            ossR = [[Reg(), Reg()], [Reg(), Reg()]]
            attn = [T("attn%d" % i, [128, 2, 512], F32) for i in range(2)]
            attnR = [Reg(), Reg()]

            def epilogue_a(t, h):
                for qb in range(2):
                    p0, p1 = 2 + (h % 2) * 2, 3 + (h % 2) * 2
                    o0 = qb * 130
                    r_, rR_ = rr[qb], rrR[qb]
                    P.op("dve", lambda e, p0=p0, o0=o0, r_=r_: e.reciprocal(out=r_[:, 0:1], in_=C.pb[p0][:, o0 + 128:o0 + 129]), reads=[C.pbR[p0]], writes=[rR_[0]])
                    P.op("dve", lambda e, p1=p1, o0=o0, r_=r_: e.reciprocal(out=r_[:, 1:2], in_=C.pb[p1][:, o0 + 128:o0 + 129]), reads=[C.pbR[p1]], writes=[rR_[1]])
                    P.op("dve", lambda e, r_=r_: e.tensor_tensor(out=r_[:, 2:3], in0=r_[:, 1:2], in1=lsc[:, 5:6], op=ALU.mult), reads=[rR_[1], cR], writes=[rR_[2]])
                    P.op("dve", lambda e, p1=p1, o0=o0, r_=r_, qb=qb: e.tensor_scalar(out=ot1[qb][:], in0=C.pb[p1][:, o0:o0 + 128], scalar1=r_[:, 2:3], scalar2=None, op0=ALU.mult),
                         reads=[C.pbR[p1], rR_[2]], writes=[ot1R[qb]])
                    P.op("dve", lambda e, p0=p0, o0=o0, r_=r_, qb=qb: e.scalar_tensor_tensor(out=oo[qb][:], in0=C.pb[p0][:, o0:o0 + 128], scalar=r_[:, 0:1], in1=ot1[qb][:],
                                                                                op0=ALU.mult, op1=ALU.add),
                         reads=[C.pbR[p0], rR_[0], ot1R[qb]], writes=[ooR[qb]])

            def epilogue_b(t, h):
                b = t % 2
                for qb in range(2):
                    P.op("act", lambda e, qb=qb: e.activation(out=oj[qb][:], in_=oo[qb][:], func=AF.Square, accum_out=oss[qb][:, 0:1]), reads=[ooR[qb]], writes=[ojR[qb], ossR[qb][0]])
                for qb in range(2):
                    P.op("pool", lambda e, qb=qb: e.tensor_scalar(out=oss[qb][:, 1:2], in0=oss[qb][:, 0:1], scalar1=1.0 / 128, scalar2=EPS, op0=ALU.mult, op1=ALU.add),
                         reads=[ossR[qb][0]], writes=[ossR[qb][1]])
                    P.op("pool", lambda e, qb=qb: e.tensor_tensor(out=oss[qb][:, 1:2], in0=oss[qb][:, 1:2], in1=C.mhalf[:, 0:1], op=ALU.pow),
                         reads=[ossR[qb][1], C.R_const], writes=[ossR[qb][1]])
                for qb in range(2):
                    P.op("dve", lambda e, qb=qb, h=h, b=b: e.scalar_tensor_tensor(out=attn[b][:, qb, h * 128:(h + 1) * 128], in0=oo[qb][:], scalar=oss[qb][:, 1:2], in1=gsub[:],
                                                                              op0=ALU.mult, op1=ALU.mult),
                         reads=[ooR[qb], ossR[qb][1], cR], writes=[attnR[b]])

            def tail(t):
                b = t % 2
                for qb in range(2):
                    for h in range(4):
                        P.op("pe", lambda e, qb=qb, h=h: e.transpose(out=C.pb[6 + qb][:, h * 128:(h + 1) * 128], in_=attn[b][:, qb, h * 128:(h + 1) * 128], identity=C.ident[:]),
                             reads=[attnR[b], C.R_const], writes=[C.pbR[6 + qb]])
                    P.op("dve", lambda e, qb=qb: e.tensor_copy(out=catT[b][:, 0:4, qb * 128:(qb + 1) * 128],
                                                             in_=C.pb[6 + qb][:, :].rearrange("p (h q) -> p h q", h=4)),
                         reads=[C.pbR[6 + qb]], writes=[catAR[b]])
                for qb in range(2):
                    for hf in range(2):
                        pbi = 6 + ((qb * 2 + hf) % 2)
                        for c in range(8):
                            P.op("pe", lambda e, c=c, qb=qb, hf=hf, pbi=pbi: e.matmul(
                                C.pb[pbi][:, :], lhsT=catT[b][:, c, qb * 128:(qb + 1) * 128], rhs=w_out[:, c, hf * 512:(hf + 1) * 512],
                                start=(c == 0), stop=(c == 7)),
                                reads=[catAR[b], catSR[b], woR], writes=[C.pbR[pbi]])
                        tb = (qb * 2 + hf) % 2
                        P.op("dve", lambda e, pbi=pbi, tb=tb, hf=hf: e.tensor_tensor(out=ty[tb][:], in0=C.pb[pbi][:, :], in1=gate[:, hf * 512:(hf + 1) * 512], op=ALU.mult),
                             reads=[C.pbR[pbi], gateR], writes=[tyR[tb]])
                        P.op("pool", lambda e, tb=tb, qb=qb, hf=hf: e.tensor_tensor(
                            out=xres[b][:, qb, hf * 512:(hf + 1) * 512], in0=xres[b][:, qb, hf * 512:(hf + 1) * 512], in1=ty[tb][:], op=ALU.add),
                            reads=[tyR[tb], xresR[b]], writes=[xresR[b]])
                P.op("sp", lambda e: e.dma_start(out=C.out[t * TT:(t + 1) * TT, :].rearrange("(s p) d -> p s d", p=128), in_=xres[b][:]),
                     reads=[xresR[b]], writes=[C.outR[t]], dma=True)

            def S(t, idx):
                b = t % 2
                h, kc = steps[idx]
                kp = idx % 2
                ktile = NT if kc >= 32 else kc // 2
                for m in range(2):
                    P.op("pe", lambda e, m=m, h=h, kc=kc, kp=kp: e.matmul(
                        C.pb[kp][:, m * TT:(m + 1) * TT], lhsT=KT[:, h, kc * 128:(kc + 1) * 128],
                        rhs=qz[m][b][:, h, :], start=True, stop=True),
                        reads=[KTR[ktile], qTR[b]], writes=[C.pbR[kp]])

            loads(0)
            loads(1)
            S(0, 0)
            for t in range(NT):
                b = t % 2
                for idx in range(nst):
                    h, kc = steps[idx]
                    kp = idx % 2
                    ktile = NT if kc >= 32 else kc // 2
                    if idx + 1 < nst:
                        S(t, idx + 1)
                    elif t + 1 < NT:
                        S(t + 1, 0)
                    P.op("act", lambda e, kp=kp: e.activation(out=pT[kp][:], in_=C.pb[kp][:, :], func=AF.Exp, bias=0.0, scale=0.125),
                         reads=[C.pbR[kp]], writes=[pTR[kp]], relax=True)
                    for m in range(2):
                        po = 2 + (h % 2) * 2 + m
                        for qb in range(2):
                            P.op("pe", lambda e, m=m, kp=kp, qb=qb, po=po, kc=kc, h=h: e.matmul(
                                C.pb[po][:, qb * 130:(qb + 1) * 130], lhsT=pT[kp][:, m * TT + qb * 128:m * TT + (qb + 1) * 128], rhs=V[:, kc, h, 0:130],
                                start=(kc == 0 and qb == 0), stop=(kc == NKC - 1), skip_group_check=True),
                                reads=[pTR[kp], VR[ktile], VoneR], writes=[C.pbR[po]])
                    if idx == 12 and t > 0:
                        tail(t - 1)
                        if t + 1 < NT:
                            loads(t + 1)
                    if kc == NKC - 1:
                        epilogue_a(t, h)
                    if idx % NKC == 6:
                        if idx >= NKC:
                            epilogue_b(t, idx // NKC - 1)
                        elif t > 0:
                            epilogue_b(t - 1, 3)
            epilogue_b(NT - 1, 3)
            tail(NT - 1)
            P.barrier()
            P.emit()


def phase_fourier(C):
    P, nc = C.P, C.nc
    l = 1
    with ExitStack() as es_o:
        hTM = C.T("hTM", [128, 32, 1024], BF16, es_o)
        hR = [Reg("hTM%d" % t) for t in range(NT)]
        with ExitStack() as es:
            T = lambda n, s, d: C.T(n, s, d, es)
            scb = T("scb", [128, 1024], F32)
            shb = T("shb", [128, 1024], F32)
            scbR, shbR = Reg(), Reg()
            gtmp = [T("gtmp%d" % i, [128, 128], F32) for i in range(2)]
            gtmpR = [Reg(), Reg()]
            xin = [T("fx%d" % i, [128, 2, 1024], F32) for i in range(2)]
            xinR = [Reg(), Reg()]
            xn2 = [T("fxn%d" % i, [128, 2, 1024], F32) for i in range(2)]
            xn2R = [Reg(), Reg()]
            sqj = T("sqj", [128, 1024], F32)
            sqjR = Reg()
            ss2 = [T("fss%d" % i, [128, 2], F32) for i in range(2)]
            rs2 = [T("frs%d" % i, [128, 2], F32) for i in range(2)]
            ss2R, rs2R = [Reg(), Reg()], [Reg(), Reg()]
            build_bc(C, shb, shbR, lambda kc: C.modcol[l][:, 3 * 8 + kc, 0:1], 1.0, pbis=(0, 1), tmp=gtmp, tmpR=gtmpR)
            build_bc(C, scb, scbR, lambda kc: C.sc1p[l][:, 4 * 8 + kc, 0:1], 1.0, pbis=(2, 3), tmp=gtmp, tmpR=gtmpR)

            def ld(t):
                P.op("sp", lambda e: e.dma_start(out=xin[t % 2][:], in_=C.out[t * TT:(t + 1) * TT, :].rearrange("(s p) d -> p s d", p=128)),
                     reads=[C.outR[t]], writes=[xinR[t % 2]], dma=True)

            ld(0)
            for t in range(NT):
                b = t % 2
                if t + 1 < NT:
                    ld(t + 1)
                xn, xnR, ss, ssR, rs, rsR = xn2[b], xn2R[b], ss2[b], ss2R[b], rs2[b], rs2R[b]
                for s in range(2):
                    P.op("act", lambda e, s=s, b=b, ss=ss: e.activation(out=sqj[:], in_=xin[b][:, s, :], func=AF.Square, accum_out=ss[:, s:s + 1]),
                         reads=[xinR[b]], writes=[sqjR, ssR], relax=True)
                P.op("pool", lambda e, rs=rs, ss=ss: e.tensor_scalar(out=rs[:], in0=ss[:], scalar1=1.0 / D, scalar2=EPS, op0=ALU.mult, op1=ALU.add),
                     reads=[ssR], writes=[rsR])
                P.op("pool", lambda e, rs=rs: e.tensor_tensor(out=rs[:], in0=rs[:], in1=C.mhalf[:, 0:2], op=ALU.pow), reads=[rsR, C.R_const], writes=[rsR])
                for s in range(2):
                    P.op("dve", lambda e, s=s, b=b, xn=xn, rs=rs: e.scalar_tensor_tensor(out=xn[:, s, :], in0=xin[b][:, s, :], scalar=rs[:, s:s + 1], in1=scb[:],
                                                                           op0=ALU.mult, op1=ALU.mult),
                         reads=[xinR[b], rsR, scbR], writes=[xnR])
                P.op("pool", lambda e, t=t, xn=xn: e.tensor_tensor(out=hTM[:, t * 2 + 0, :], in0=xn[:, 0, :], in1=shb[:], op=ALU.add),
                     reads=[xnR, shbR], writes=[hR[t]])
                P.op("dve", lambda e, t=t, xn=xn: e.tensor_tensor(out=hTM[:, t * 2 + 1, :], in0=xn[:, 1, :], in1=shb[:], op=ALU.add),
                     reads=[xnR, shbR], writes=[hR[t]])
            P.barrier()
            P.emit()
        with ExitStack() as es:
            T = lambda n, s, d: C.T(n, s, d, es)
            NKT = NT // 2
            Ct = [T("Ct%d" % i, [128, 32, TT], BF16) for i in range(2)]
            St = [T("St%d" % i, [128, 32, TT], BF16) for i in range(2)]
            CtR = [Reg(), Reg()]
            StR = [Reg(), Reg()]
            T0 = T("T0", [128, 32, 2], BF16)
            ZT = T("ZT", [128, 16, TT], BF16)
            ZTR = [Reg("ZT%d" % i) for i in range(16)]
            fT = [T("fT%d" % i, [128, 8, TT], BF16) for i in range(2)]
            fTR = [Reg("fT0"), Reg("fT1")]
            wfo = T("wfo", [128, 8, 1024], BF16)
            wfoR = Reg()
            ccC = T("ccC", [128, 2, 256], BF16)
            ccS = T("ccS", [128, 2, 256], BF16)
            ccSn = T("ccSn", [128, 2, 256], BF16)
            cR = Reg()
            gate = T("gate5", [128, 1024], F32)
            gateR = Reg()
            gb = T("gb", [128, 1024], F32)
            gbR = Reg()
            gtmp = [T("gtmp%d" % i, [128, 128], F32) for i in range(2)]
            gtmpR = [Reg(), Reg()]
            xres = [T("xres%d" % i, [128, 2, 1024], F32) for i in range(3)]
            xresR = [Reg(), Reg(), Reg()]
            ty = [T("ty%d" % i, [128, 512], F32) for i in range(2)]
            tyR = [Reg(), Reg()]
            x0 = T("x0", [1, 1024], F32)
            x0R = Reg()
            Z0 = T("Z0", [128, 8, 2], BF16)
            Z0R = Reg()
            f0 = T("f0", [128, 8, 2], BF16)
            f0R = Reg()

            wf_src = C.w_fo.rearrange("(c p) d -> p c d", p=128)
            for hh in range(2):
                P.op("pool", lambda e, hh=hh: e.dma_start(out=wfo[:, hh * 4:(hh + 1) * 4, :], in_=wf_src[:, hh * 4:(hh + 1) * 4, :]),
                     writes=[wfoR], dma=True)
            P.op("sp", lambda e: e.dma_start(out=ccC[:], in_=C.ccC[:, :, :]), writes=[cR], dma=True)
            P.op("sp", lambda e: e.dma_start(out=ccS[:], in_=C.ccS[:, :, :]), writes=[cR], dma=True)
            P.op("sp", lambda e: e.dma_start(out=ccSn[:], in_=C.ccSn[:, :, :]), writes=[cR], dma=True)
            P.op("sp", lambda e: e.dma_start(out=T0[:], in_=C.dft0[:, :, :]), writes=[cR], dma=True)
            P.op("sp", lambda e: e.dma_start(out=x0[:], in_=C.out[0:1, :]), writes=[x0R], dma=True)
            P.op("sp", lambda e: e.dma_start(out=gb[:], in_=C.b_fo[0:1, :].partition_broadcast(128)), writes=[gbR], dma=True)
            build_bc(C, gate, gateR, lambda kc: C.modcol[l][:, 5 * 8 + kc, 0:1], 1.0, pbis=(6, 7), tmp=gtmp, tmpR=gtmpR)
            P.op("pool", lambda e: e.tensor_tensor(out=gb[:], in0=gb[:], in1=gate[:], op=ALU.mult), reads=[gbR, gateR], writes=[gbR])

            def rows(q):
                kt, v = q // 2, q % 2
                if v == 0:
                    r0 = kt * TT + 1
                    return C.out[r0:r0 + TT, :].rearrange("(s p) d -> p s d", p=128), [C.outR[kt], C.outR[min(kt + 1, NT - 1)]]
                base = (NT - 1 - kt) * TT
                return C.out[base:base + TT, :].rearrange("(s p) d -> p s d", p=128), [C.outR[NT - 1 - kt]]

            def ldx(q):
                ap_, regs = rows(q)
                P.op("sp", lambda e: e.dma_start(out=xres[q % 3][:], in_=ap_), reads=regs, writes=[xresR[q % 3]], dma=True)

            def ldt(kt):
                b = kt % 2
                P.op("sp", lambda e: e.dma_start(out=Ct[b][:], in_=C.dftC[kt]), writes=[CtR[b]], dma=True)
                P.op("act", lambda e: e.dma_start(out=St[b][:], in_=C.dftS[kt]), writes=[StR[b]], dma=True)

            for cc in range(8):
                for n_ in range(32):
                    P.op("pe", lambda e, n_=n_, cc=cc: e.matmul(C.pb[0][:, 0:2], lhsT=hTM[:, n_, cc * 128:(cc + 1) * 128], rhs=T0[:, n_, :],
                                                              start=(n_ == 0), stop=(n_ == 31)),
                         reads=[hR[n_ // 2], cR], writes=[C.pbR[0]])
                P.op("act", lambda e, cc=cc: e.activation(out=Z0[:, cc, :], in_=C.pb[0][:, 0:2], func=AF.Identity), reads=[C.pbR[0]], writes=[Z0R])
            for cg in range(4):
                for cj in range(2):
                    for ci in range(2):
                        P.op("pe", lambda e, cg=cg, cj=cj, ci=ci: e.matmul(C.pb[1][:, 0:2], lhsT=ccC[:, ci, cj * 128:(cj + 1) * 128], rhs=Z0[:, cg * 2 + ci, :],
                                                                         start=(ci == 0), stop=(ci == 1)),
                             reads=[cR, Z0R], writes=[C.pbR[1]])
                    P.op("act", lambda e, cg=cg, cj=cj: e.activation(out=f0[:, cg * 2 + cj, :], in_=C.pb[1][:, 0:2], func=AF.Identity),
                         reads=[C.pbR[1]], writes=[f0R])
            for hf in range(2):
                for c in range(8):
                    P.op("pe", lambda e, c=c, hf=hf: e.matmul(C.pb[2 + hf][0:2, :], lhsT=f0[:, c, :], rhs=wfo[:, c, hf * 512:(hf + 1) * 512],
                                                            start=(c == 0), stop=(c == 7)),
                         reads=[f0R, wfoR], writes=[C.pbR[2 + hf]])
                P.op("dve", lambda e, hf=hf: e.tensor_tensor(out=ty[hf][0:1, :], in0=C.pb[2 + hf][0:1, :], in1=gate[0:1, hf * 512:(hf + 1) * 512], op=ALU.mult),
                     reads=[C.pbR[2 + hf], gateR], writes=[tyR[hf]])
                P.op("pool", lambda e, hf=hf: e.tensor_tensor(out=x0[0:1, hf * 512:(hf + 1) * 512], in0=x0[0:1, hf * 512:(hf + 1) * 512], in1=ty[hf][0:1, :], op=ALU.add),
                     reads=[tyR[hf], x0R], writes=[x0R])
            P.op("pool", lambda e: e.tensor_tensor(out=x0[0:1, :], in0=x0[0:1, :], in1=gb[0:1, :], op=ALU.add), reads=[x0R, gbR], writes=[x0R])
            P.op("sp", lambda e: e.dma_start(out=C.out[0:1, :], in_=x0[:]), reads=[x0R], dma=True)

            ldt(0)
            ldx(0)
            for kt in range(NKT):
                b = kt % 2
                if kt + 1 < NKT:
                    ldt(kt + 1)
                for cc in range(8):
                    pr = (cc % 2) * 2
                    pi = pr + 1
                    for n_ in range(32):
                        P.op("pe", lambda e, n_=n_, cc=cc, pr=pr, b=b: e.matmul(
                            C.pb[pr][:, 0:TT], lhsT=hTM[:, n_, cc * 128:(cc + 1) * 128], rhs=Ct[b][:, n_, :], start=(n_ == 0), stop=(n_ == 31)),
                            reads=[hR[n_ // 2], CtR[b]], writes=[C.pbR[pr]])
                        P.op("pe", lambda e, n_=n_, cc=cc, pi=pi, b=b: e.matmul(
                            C.pb[pi][:, 0:TT], lhsT=hTM[:, n_, cc * 128:(cc + 1) * 128], rhs=St[b][:, n_, :], start=(n_ == 0), stop=(n_ == 31)),
                            reads=[hR[n_ // 2], StR[b]], writes=[C.pbR[pi]])
                    P.op("act", lambda e, cc=cc, pr=pr: e.activation(out=ZT[:, cc, :], in_=C.pb[pr][:, 0:TT], func=AF.Identity),
                         reads=[C.pbR[pr]], writes=[ZTR[cc]])
                    P.op("dve", lambda e, cc=cc, pi=pi: e.tensor_copy(out=ZT[:, 8 + cc, :], in_=C.pb[pi][:, 0:TT]),
                         reads=[C.pbR[pi]], writes=[ZTR[8 + cc]])
                for v in range(2):
                    q = kt * 2 + v
                    xb = q % 3
                    if q + 1 < 2 * NKT:
                        ldx(q + 1)
                    wS = ccS if v == 0 else ccSn
                    for cg in range(4):
                        for cj in range(2):
                            pf = 4 + ((cg * 2 + cj) % 2)
                            ops4 = [(ccC, 0, cg * 2 + 0), (ccC, 1, cg * 2 + 1), (wS, 0, 8 + cg * 2 + 0), (wS, 1, 8 + cg * 2 + 1)]
                            for qq, (wm, ci, zi) in enumerate(ops4):
                                P.op("pe", lambda e, wm=wm, ci=ci, zi=zi, cj=cj, pf=pf, qq=qq: e.matmul(
                                    C.pb[pf][:, 0:TT], lhsT=wm[:, ci, cj * 128:(cj + 1) * 128], rhs=ZT[:, zi, :], start=(qq == 0), stop=(qq == 3)),
                                    reads=[cR, ZTR[zi]], writes=[C.pbR[pf]])
                            if v == 0:
                                P.op("act", lambda e, cg=cg, cj=cj, pf=pf: e.activation(out=fT[0][:, cg * 2 + cj, :], in_=C.pb[pf][:, 0:TT], func=AF.Identity),
                                     reads=[C.pbR[pf]], writes=[fTR[0]])
                            else:
                                P.op("act", lambda e, cg=cg, cj=cj, pf=pf: e.activation(out=fT[1][:, cg * 2 + cj, ::-1], in_=C.pb[pf][:, 0:TT], func=AF.Identity),
                                     reads=[C.pbR[pf]], writes=[fTR[1]])
                    for s_ in range(2):
                        for hf in range(2):
                            pbi = 6 + ((s_ * 2 + hf) % 2)
                            for c in range(8):
                                P.op("pe", lambda e, c=c, s_=s_, hf=hf, pbi=pbi, v=v: e.matmul(
                                    C.pb[pbi][:, :], lhsT=fT[v][:, c, s_ * 128:(s_ + 1) * 128], rhs=wfo[:, c, hf * 512:(hf + 1) * 512], start=(c == 0), stop=(c == 7)),
                                    reads=[fTR[v], wfoR], writes=[C.pbR[pbi]])
                            tb = (s_ * 2 + hf) % 2
                            P.op("dve", lambda e, pbi=pbi, tb=tb, hf=hf: e.tensor_tensor(out=ty[tb][:], in0=C.pb[pbi][:, :], in1=gate[:, hf * 512:(hf + 1) * 512], op=ALU.mult),
                                 reads=[C.pbR[pbi], gateR], writes=[tyR[tb]])
                            P.op("pool", lambda e, tb=tb, s_=s_, hf=hf, xb=xb: e.tensor_tensor(
                                out=xres[xb][:, s_, hf * 512:(hf + 1) * 512], in0=xres[xb][:, s_, hf * 512:(hf + 1) * 512], in1=ty[tb][:], op=ALU.add),
                                reads=[tyR[tb], xresR[xb]], writes=[xresR[xb]])
                        P.op("pool", lambda e, s_=s_, xb=xb: e.tensor_tensor(out=xres[xb][:, s_, :], in0=xres[xb][:, s_, :], in1=gb[:], op=ALU.add),
                             reads=[xresR[xb], gbR], writes=[xresR[xb]])
                    ap_, regs = rows(q)
                    if v == 0 and kt == NKT - 1:
                        r0 = kt * TT + 1
                        P.op("sp", lambda e, xb=xb, r0=r0: e.dma_start(out=C.out[r0:r0 + 128, :], in_=xres[xb][:, 0, :]),
                             reads=[xresR[xb]], writes=regs, dma=True)
                        P.op("sp", lambda e, xb=xb, r0=r0: e.dma_start(out=C.out[r0 + 128:r0 + 255, :], in_=xres[xb][0:127, 1, :]),
                             reads=[xresR[xb]], writes=regs, dma=True)
                    else:
                        P.op("sp", lambda e, xb=xb, ap_=ap_: e.dma_start(out=ap_, in_=xres[xb][:]), reads=[xresR[xb]], writes=regs, dma=True)
            P.barrier()
            P.emit()


def _consts():
    ident = np.eye(128, dtype=np.float32)
    F = 16
    inv = (10000.0 ** (-np.arange(F, dtype=np.float32) / F)).astype(np.float32)
    tok = np.arange(N)
    pos = [tok // 64, tok % 64]
    rc = np.zeros((128, N), np.float32)
    rs = np.zeros((128, N), np.float32)
    for m in range(2):
        for ax in range(2):
            ang = (pos[ax][None, :].astype(np.float32) * inv[:, None]).astype(np.float32)
            for hf in range(2):
                p0 = m * 64 + ax * 32 + hf * 16
                rc[p0:p0 + F] = np.cos(ang)
                rs[p0:p0 + F] = np.sin(ang) * (-1.0 if hf == 0 else 1.0)
    n = np.arange(N, dtype=np.int64)
    kk = np.arange(1, N // 2 + 1, dtype=np.int64)
    nk = (n[:, None] * kk[None, :]) % N
    ang = 2.0 * np.pi * nk.astype(np.float64) / N
    Cm = (np.cos(ang) / 64.0)
    Sm = (-np.sin(ang) / 64.0)

    def lay(M):
        return np.ascontiguousarray(M.reshape(32, 128, NT // 2, TT).transpose(2, 1, 0, 3)).astype(ml_dtypes.bfloat16)

    dC, dS = lay(Cm), lay(Sm)
    d0 = np.full((128, 32, 2), 1.0 / 64.0, dtype=np.float64).astype(ml_dtypes.bfloat16)
    c = np.arange(256, dtype=np.int64)
    a2 = 2.0 * np.pi * ((c[:, None] * c[None, :]) % 256).astype(np.float64) / 256
    cC = np.cos(a2) / 16.0
    cS = np.sin(a2) / 16.0

    def lay2(M):
        return np.ascontiguousarray(M.reshape(2, 128, 256).transpose(1, 0, 2)).astype(ml_dtypes.bfloat16)

    return dict(ident=ident, rope_c=rc, rope_s=rs, dftC=dC, dftS=dS, dft0=d0, ccC=lay2(cC), ccS=lay2(cS), ccSn=lay2(-cS))


def _rot_perm():
    idx = np.arange(1024).reshape(2, 4, 2, 2, 2, 16)
    return idx[:, :, :, :, ::-1, :].reshape(-1)


def make_in_maps(inputs, n_cores=8):
    f = lambda a: np.ascontiguousarray(np.asarray(a, dtype=np.float32))
    x, c, ctx, c_ctx = f(inputs["x"]), f(inputs["c"]), f(inputs["ctx"]), f(inputs["c_ctx"])
    consts = _consts()
    w_in = f(inputs["mix_w_in"])[0]
    perm = _rot_perm()
    shared = dict(
        ada_w=f(inputs["ada_w"]),
        ada_bc=np.ascontiguousarray(f(inputs["ada_b"]).reshape(2, 72, 128).transpose(0, 2, 1)),
        w_gu=f(inputs["ffn_w_gu"]), w_dn=f(inputs["ffn_w_down"]),
        w_in=w_in, w_rot=np.ascontiguousarray(w_in[:, :1024][:, perm]),
        w_out=f(inputs["mix_w_out"])[0],
        lam=f(inputs["diff_lambda"]).reshape(1, 256),
        subg=f(inputs["diff_subln_g"]).reshape(1, 128),
        sgu_g=f(inputs["sgu_norm_g"]).reshape(1, 512), sgu_bn=f(inputs["sgu_norm_b"]).reshape(1, 512),
        sgu_wT=np.ascontiguousarray(f(inputs["sgu_w"])[0].transpose(0, 2, 1)),
        sgu_b=f(inputs["sgu_b"]).reshape(1, 512),
        w_fo=f(inputs["fourier_w_out"])[0], b_fo=f(inputs["fourier_b_out"]).reshape(1, 1024),
        fng=f(inputs["final_norm_g"]).reshape(1, 1024),
        **consts,
    )
    maps = []
    for b in range(n_cores):
        cc = np.stack([c[b], c_ctx], axis=-1)
        cc = np.ascontiguousarray(cc.reshape(8, 128, 2).transpose(1, 0, 2))
        m = dict(shared)
        m.update(x=x[b], ctx=ctx[b], cc=cc)
        maps.append(m)
    return maps


def kernel(**inputs):
    nc = build()
    maps = make_in_maps(inputs, 8)
    res = run_bass_kernel_spmd(nc, maps, core_ids=list(range(8)))
    return np.stack([np.asarray(r["out"], dtype=np.float32) for r in res.results], axis=0)
```
